# Optimizing a Trainium2 kernel written in Bass

```python
import math
import jax, jax.numpy as jnp
from jax import lax
import numpy as np

D_MODEL = 1024
BATCH = 32
SEQ = 256
DEPTH = 4
DEC_BATCH = 8
DEC_SEQ = 2048
PAST_LEN = 256

GRID_W = 64
BLK = 128
WINDOW = 128
ROPE_BASE = 10000.0
EPS = 1e-6
NEG_INF = -1e30
D_MIX = D_MODEL
A_WIDTH = D_MIX // 2
B_WIDTH = D_MIX // 4
C_WIDTH = D_MIX // 4
A_HEADS = 4
A_V_DIM = A_WIDTH // A_HEADS
A_QK_DIM = A_V_DIM // 2
B_HEADS = 4
B_KV_HEADS = 2
B_GROUP = B_HEADS // B_KV_HEADS
B_HEAD_DIM = B_WIDTH // B_HEADS
C_GROUPS = 4
C_GROUP_DIM = C_WIDTH // C_GROUPS
A_Q_COLS = A_HEADS * 2 * A_QK_DIM
A_K_COLS = A_HEADS * 2 * A_QK_DIM
A_V_COLS = A_HEADS * A_V_DIM
B_Q_COLS = B_HEADS * B_HEAD_DIM
B_KV_COLS = B_KV_HEADS * B_HEAD_DIM
GATE_COLS = D_MIX
IN_COLS = A_Q_COLS + A_K_COLS + A_V_COLS + B_Q_COLS + 2 * B_KV_COLS + C_WIDTH + GATE_COLS

kernel_name = "hybrid_diffattn_swa_fourier_prefix_dit_step"


def split_points():
    sizes = [A_Q_COLS, A_K_COLS, A_V_COLS, B_Q_COLS, B_KV_COLS, B_KV_COLS, C_WIDTH]
    return [int(v) for v in np.cumsum(sizes)]


def rms_norm(x, g):
    xf = x.astype(jnp.float32)
    y = xf * lax.rsqrt(jnp.mean(xf * xf, axis=-1, keepdims=True) + EPS)
    return (y * g.astype(jnp.float32)).astype(x.dtype)


def axial_rope_tables(n_tokens, dim, dtype):
    rows = n_tokens // GRID_W
    row = jnp.repeat(jnp.arange(rows), GRID_W).astype(jnp.float32)
    col = jnp.tile(jnp.arange(GRID_W), rows).astype(jnp.float32)
    nf = dim // 4
    inv = ROPE_BASE ** (-jnp.arange(nf, dtype=jnp.float32) / nf)
    ar = row[:, None] * inv
    ac = col[:, None] * inv
    ang = jnp.concatenate([ar, ar, ac, ac], axis=-1)
    return jnp.cos(ang).astype(dtype), jnp.sin(ang).astype(dtype)


def apply_rope(x, cos, sin):
    d = x.shape[-1]
    shape = (x.shape[1],) + (1,) * (x.ndim - 3) + (d,)
    c = cos.reshape(shape)
    s = sin.reshape(shape)
    x1a, x1b, x2a, x2b = jnp.split(x, 4, axis=-1)
    rot = jnp.concatenate([-x1b, x1a, -x2b, x2a], axis=-1)
    return x * c + rot * s


def diff_attention(q, k, v, lam):
    b, sq, h, _, dk = q.shape
    nb = sq // BLK
    qb = jnp.moveaxis(q.reshape(b, nb, BLK, h, 2, dk), 1, 0)
    scale = dk ** -0.5

    def one(qblk):
        s = jnp.einsum('bqhcd,bkhcd->bhcqk', qblk, k).astype(jnp.float32) * scale
        p = jax.nn.softmax(s, axis=-1)
        w = p[:, :, 0] - lam * p[:, :, 1]
        return jnp.einsum('bhqk,bkhd->bqhd', w.astype(v.dtype), v)

    o = lax.map(one, qb)
    return jnp.moveaxis(o, 0, 1).reshape(b, sq, h, v.shape[-1])


def sink_attention(q, k, v, sink):
    b, sq, kvh, g, d = q.shape
    nb = sq // BLK
    qb = jnp.moveaxis(q.reshape(b, nb, BLK, kvh, g, d), 1, 0)
    scale = d ** -0.5

    def one(qblk):
        s = jnp.einsum('bqkgd,bjkd->bkgqj', qblk, k).astype(jnp.float32) * scale
        sk = jnp.broadcast_to(sink.astype(jnp.float32)[None, :, :, None, None], s.shape[:-1] + (1,))
        p = jax.nn.softmax(jnp.concatenate([s, sk], axis=-1), axis=-1)[..., :-1]
        return jnp.einsum('bkgqj,bjkd->bqkgd', p.astype(v.dtype), v)

    o = lax.map(one, qb)
    return jnp.moveaxis(o, 0, 1).reshape(b, sq, kvh * g * d)


def banded_sink_attention(q, k, v, kc, vc, sink):
    b, s_len, kvh, g, d = q.shape
    nb = s_len // BLK
    qb = q.reshape(b, nb, BLK, kvh, g, d)

    def windows(t):
        tb = t.reshape(b, nb, BLK, kvh, d)
        tp = jnp.pad(tb, ((0, 0), (1, 1), (0, 0), (0, 0), (0, 0)))
        return jnp.concatenate([tp[:, :-2], tp[:, 1:-1], tp[:, 2:]], axis=2)

    kw, vw = windows(k), windows(v)
    scale = d ** -0.5
    s_loc = jnp.einsum('bnqkgd,bnjkd->bnkgqj', qb, kw).astype(jnp.float32) * scale
    blk = jnp.arange(nb)[:, None, None]
    qa = jnp.arange(BLK)[None, :, None]
    kj = jnp.arange(3 * BLK)[None, None, :]
    kpos = blk * BLK + kj - BLK
    qpos = blk * BLK + qa
    valid = (jnp.abs(kpos - qpos) <= WINDOW) & (kpos >= 0) & (kpos < s_len)
    s_loc = jnp.where(valid[None, :, None, None], s_loc, NEG_INF)
    s_ctx = jnp.einsum('bnqkgd,bjkd->bnkgqj', qb, kc).astype(jnp.float32) * scale
    sk = jnp.broadcast_to(sink.astype(jnp.float32)[None, None, :, :, None, None], s_loc.shape[:-1] + (1,))
    p = jax.nn.softmax(jnp.concatenate([s_loc, s_ctx, sk], axis=-1), axis=-1)
    p_loc = p[..., :3 * BLK].astype(v.dtype)
    p_ctx = p[..., 3 * BLK:-1].astype(v.dtype)
    o = (jnp.einsum('bnkgqj,bnjkd->bnqkgd', p_loc, vw)
         + jnp.einsum('bnkgqj,bjkd->bnqkgd', p_ctx, vc))
    return o.reshape(b, s_len, kvh * g * d)


def fourier_mix(u, w_f):
    b, s, _ = u.shape
    ug = u.reshape(b, s, C_GROUPS, C_GROUP_DIM).astype(jnp.float32)
    f = jnp.fft.fft2(ug, axes=(1, 3), norm="ortho").real.astype(u.dtype)
    return jnp.einsum('bsgc,gcd->bsgd', f, w_f).reshape(b, s, C_WIDTH)


def mixer_layer(x, shift, scale, gate, norm_g, w_in, lam, lam_init, diff_g, sink, w_f, w_out,
                rope_a=None, rope_b=None, ctx=None):
    b, s, _ = x.shape
    h = rms_norm(x, norm_g) * (1.0 + scale) + shift
    proj = h @ w_in
    qa, ka, va, qbq, kbk, vbv, uc, z = jnp.split(proj, split_points(), axis=-1)
    qa = qa.reshape(b, s, A_HEADS, 2, A_QK_DIM)
    ka = ka.reshape(b, s, A_HEADS, 2, A_QK_DIM)
    va = va.reshape(b, s, A_HEADS, A_V_DIM)
    qbq = qbq.reshape(b, s, B_KV_HEADS, B_GROUP, B_HEAD_DIM)
    kbk = kbk.reshape(b, s, B_KV_HEADS, B_HEAD_DIM)
    vbv = vbv.reshape(b, s, B_KV_HEADS, B_HEAD_DIM)
    if ctx is None:
        oa = diff_attention(qa, ka, va, lam)
        ob = sink_attention(qbq, kbk, vbv, sink)
        new_ctx = (ka.reshape(b, s, A_HEADS, 2 * A_QK_DIM), va, kbk, vbv)
    else:
        cos_a, sin_a = rope_a
        cos_b, sin_b = rope_b
        kc_a, vc_a, kc_b, vc_b = ctx
        n_ctx = kc_a.shape[1]
        k_all = jnp.concatenate([apply_rope(ka, cos_a, sin_a),
                                 kc_a.reshape(b, n_ctx, A_HEADS, 2, A_QK_DIM)], axis=1)
        v_all = jnp.concatenate([va, vc_a], axis=1)
        oa = diff_attention(apply_rope(qa, cos_a, sin_a), k_all, v_all, lam)
        ob = banded_sink_attention(apply_rope(qbq, cos_b, sin_b), apply_rope(kbk, cos_b, sin_b),
                                   vbv, kc_b, vc_b, sink)
        new_ctx = None
    oa = (rms_norm(oa, diff_g) * (1.0 - lam_init)).reshape(b, s, A_WIDTH)
    oc = fourier_mix(uc, w_f)
    mix = jnp.concatenate([oa, ob, oc], axis=-1) * jax.nn.silu(z)
    return x + gate * (mix @ w_out), new_ctx


def setup_inputs(seed: int = 0) -> dict:
    key = jax.random.key(seed)
    ks = jax.random.split(key, 24)
    f32 = jnp.float32
    n = lambda k, shp: jax.random.normal(k, shp, dtype=f32)
    return {
        "x_prompt": n(ks[0], (BATCH, SEQ, D_MODEL)),
        "x_sample": n(ks[1], (DEC_BATCH, DEC_SEQ, D_MODEL)),
        "cache_diff_k": n(ks[2], (DEC_BATCH, DEPTH, PAST_LEN, A_HEADS, 2 * A_QK_DIM)),
        "cache_diff_v": n(ks[3], (DEC_BATCH, DEPTH, PAST_LEN, A_HEADS, A_V_DIM)),
        "cache_swa_k": n(ks[4], (DEC_BATCH, DEPTH, PAST_LEN, B_KV_HEADS, B_HEAD_DIM)),
        "cache_swa_v": n(ks[5], (DEC_BATCH, DEPTH, PAST_LEN, B_KV_HEADS, B_HEAD_DIM)),
        "c": n(ks[6], (DEC_BATCH, D_MODEL)),
        "c_ctx": n(ks[7], (D_MODEL,)),
        "w_ada": n(ks[8], (DEPTH, D_MODEL, 3 * D_MODEL)) * D_MODEL ** -0.5,
        "b_ada": n(ks[9], (DEPTH, 3 * D_MODEL)) * 0.01,
        "norm_g": 1.0 + 0.1 * n(ks[10], (DEPTH, D_MODEL)),
        "w_in": n(ks[11], (DEPTH, D_MODEL, IN_COLS)) * D_MODEL ** -0.5,
        "lam_q1": n(ks[12], (DEPTH, A_QK_DIM)) * 0.1,
        "lam_k1": n(ks[13], (DEPTH, A_QK_DIM)) * 0.1,
        "lam_q2": n(ks[14], (DEPTH, A_QK_DIM)) * 0.1,
        "lam_k2": n(ks[15], (DEPTH, A_QK_DIM)) * 0.1,
        "diff_norm_g": 1.0 + 0.1 * n(ks[16], (DEPTH, A_V_DIM)),
        "sink": n(ks[17], (DEPTH, B_KV_HEADS, B_GROUP)) * 0.5,
        "w_fourier": n(ks[18], (DEPTH, C_GROUPS, C_GROUP_DIM, C_GROUP_DIM)) * C_GROUP_DIM ** -0.5,
        "w_out": n(ks[19], (DEPTH, D_MIX, D_MODEL)) * D_MIX ** -0.5,
        "final_g": 1.0 + 0.1 * n(ks[20], (D_MODEL,)),
    }


def reference(x_prompt, x_sample, cache_diff_k, cache_diff_v, cache_swa_k, cache_swa_v, c, c_ctx,
              w_ada, b_ada, norm_g, w_in, lam_q1, lam_k1, lam_q2, lam_k2, diff_norm_g, sink,
              w_fourier, w_out, final_g):
    n_lat = x_sample.shape[1]
    rope_a = axial_rope_tables(n_lat, A_QK_DIM, x_sample.dtype)
    rope_b = axial_rope_tables(n_lat, B_HEAD_DIM, x_sample.dtype)
    xc = x_prompt
    xs = x_sample
    new_dk, new_dv, new_sk, new_sv = [], [], [], []
    for l in range(DEPTH):
        lam_init = 0.8 - 0.6 * math.exp(-0.3 * l)
        lam = (jnp.exp(jnp.sum(lam_q1[l].astype(jnp.float32) * lam_k1[l].astype(jnp.float32)))
               - jnp.exp(jnp.sum(lam_q2[l].astype(jnp.float32) * lam_k2[l].astype(jnp.float32)))
               + lam_init)
        mod_c = jax.nn.silu(c_ctx) @ w_ada[l] + b_ada[l]
        sh_c, sc_c, g_c = jnp.split(mod_c, 3, axis=-1)
        xc, (dk, dv, sk, sv) = mixer_layer(xc, sh_c, sc_c, g_c, norm_g[l], w_in[l], lam, lam_init,
                                           diff_norm_g[l], sink[l], w_fourier[l], w_out[l])
        new_dk.append(dk)
        new_dv.append(dv)
        new_sk.append(sk)
        new_sv.append(sv)
        mod_s = (jax.nn.silu(c) @ w_ada[l] + b_ada[l])[:, None, :]
        sh_s, sc_s, g_s = jnp.split(mod_s, 3, axis=-1)
        ctx = (cache_diff_k[:, l], cache_diff_v[:, l], cache_swa_k[:, l], cache_swa_v[:, l])
        xs, _ = mixer_layer(xs, sh_s, sc_s, g_s, norm_g[l], w_in[l], lam, lam_init,
                            diff_norm_g[l], sink[l], w_fourier[l], w_out[l],
                            rope_a=rope_a, rope_b=rope_b, ctx=ctx)
    y_prompt = rms_norm(xc, final_g)
    y_sample = rms_norm(xs, final_g)
    new_diff_k = jnp.stack(new_dk, axis=1)
    new_diff_v = jnp.stack(new_dv, axis=1)
    new_swa_k = jnp.stack(new_sk, axis=1)
    new_swa_v = jnp.stack(new_sv, axis=1)
    return (y_prompt, y_sample, new_diff_k, new_diff_v, new_swa_k, new_swa_v)
```

```python
import contextlib
import math

import ml_dtypes
import numpy as np

import concourse.bass as bass
import concourse.mybir as mybir
from concourse.bass_utils import run_bass_kernel_spmd

F32 = mybir.dt.float32
BF16 = mybir.dt.bfloat16
AF = mybir.ActivationFunctionType
ALU = mybir.AluOpType
AX = mybir.AxisListType

NL = 4
D = 1024
KD = 8
IN_COLS = 3328
AQ0, AK0, AV0, BQ0, BK0, BV0, UC0, Z0 = 0, 512, 1024, 1536, 1792, 1920, 2048, 2304
S_P, NSEQ_P, T_P = 256, 4, 1024
S_S, T_S, NCTX = 2048, 2048, 256
EPS = 1e-6
NEG = -30000.0
SCALE = 0.125


class Sem:
    def __init__(self, h, pool=None):
        self.h = h
        self.cnt = 0
        self.pool = pool


class Res:
    __slots__ = ("name", "w", "r", "ld", "st", "arena", "excl")

    def __init__(self, name, arena=False, excl=False):
        self.name = name
        self.arena = arena
        self.excl = excl
        self.w = None
        self.r = []
        self.ld = None
        self.st = None


class Op:
    __slots__ = ("eng", "fn", "kw", "deps", "is_dma", "signal", "cnt", "sem", "snap", "idx")

    def __init__(self, eng, fn, kw):
        self.eng = eng
        self.fn = fn
        self.kw = kw
        self.deps = []
        self.is_dma = False
        self.signal = False
        self.cnt = 0
        self.sem = None
        self.snap = None


class Prog:
    def __init__(self, nc, es):
        self.nc = nc
        self.ops = []
        self.engs = {"pe": nc.tensor, "act": nc.scalar, "dve": nc.vector, "pool": nc.gpsimd, "sp": nc.sync}
        self.esem = {e: Sem(es.enter_context(nc.semaphore("e_" + e))) for e in ("pe", "act", "dve", "pool")}
        self.free_sems = []
        self.free_sems.extend(Sem(es.enter_context(nc.semaphore(f"d{i}")), self.free_sems) for i in range(64))
        self.sw_sems = []
        self.sw_sems.extend(Sem(es.enter_context(nc.semaphore(f"w{i}")), self.sw_sems) for i in range(8))
        self.perm_sems = [Sem(es.enter_context(nc.semaphore(f"q{i}"))) for i in range(20)]
        self.st_sems = []
        self.marks = []

    def mark(self, label):
        self.marks.append((label, len(self.ops)))

    def op(self, eng, fn, kw, reads=(), writes=()):
        o = Op(eng, fn, kw)
        o.idx = len(self.ops)
        deps = []
        if any(R.excl for R in reads):
            writes = list(writes) + [R for R in reads if R.excl and R not in writes]
            reads = [R for R in reads if not R.excl]
        for R in reads:
            if R.w is not None:
                deps.append(R.w)
        for R in writes:
            if R.w is not None:
                deps.append(R.w)
            deps.extend(R.r)
        for e in deps:
            if e[0] == "c":
                if e[1].eng == eng and eng == "pe":
                    continue
                e[1].signal = True
            o.deps.append(e)
        ev = ("c", o)
        for R in writes:
            R.w = ev
            R.r = []
        for R in reads:
            if R.w is not ev:
                R.r = [e for e in R.r if not (e[0] == "c" and e[1].eng == eng)]
                R.r.append(ev)
        self.ops.append(o)
        return o

    def dma(self, q, out, in_, res, kind):
        engs = self.engs
        o = Op(q, engs[q].dma_start, dict(out=out, in_=in_))
        o.idx = len(self.ops)
        o.is_dma = True
        deps = []
        if res.w is not None:
            deps.append(res.w)
        if kind == "ld":
            deps.extend(res.r)
        for e in deps:
            if e[0] == "c":
                e[1].signal = True
            o.deps.append(e)
        if kind == "ld":
            if res.ld is None:
                res.ld = ((self.sw_sems if q == "pool" else self.free_sems) if res.arena else self.perm_sems).pop()
            sem = res.ld
        else:
            if res.st is None:
                res.st = (self.free_sems if res.arena else self.perm_sems).pop()
                self.st_sems.append(res.st)
            sem = res.st
        sem.cnt += 16
        ev = ("d", sem, sem.cnt)
        o.sem = sem
        if kind == "ld":
            res.w = ev
            res.r = []
        else:
            res.r.append(ev)
        self.ops.append(o)
        return o

    def retire(self, ress, new_ress):
        evs = []
        for R in ress:
            if R.w is not None:
                evs.append(R.w)
            evs.extend(R.r)
            for s in (R.ld, R.st):
                if s is not None:
                    s.pool.append(s)
            R.ld = R.st = None
        best = {}
        for e in evs:
            if e[0] == "c":
                key = ("c", e[1].eng)
                cur = best.get(key)
                if cur is None or cur[1].idx < e[1].idx:
                    best[key] = e
            else:
                key = ("d", id(e[1]))
                cur = best.get(key)
                if cur is None or cur[2] < e[2]:
                    best[key] = e
        evs = list(best.values())
        for R in new_ress:
            R.r = list(evs)

    def emit(self):
        seen = {e: {} for e in self.engs}
        cnt = {e: 0 for e in self.engs}
        nwait = 0
        self.inst_names = []
        for o in self.ops:
            eng = self.engs[o.eng]
            sn = seen[o.eng]
            need = {}
            for e in o.deps:
                if e[0] == "c":
                    s, v, snap = self.esem[e[1].eng], e[1].cnt, e[1].snap
                else:
                    s, v, snap = e[1], e[2], None
                cur = need.get(s)
                if cur is None or cur[0] < v:
                    need[s] = (v, snap)
            for s, (v, snap) in need.items():
                if sn.get(s, 0) < v:
                    eng.wait_ge(s.h, v)
                    nwait += 1
                    sn[s] = v
                if snap is not None:
                    for k2, v2 in snap.items():
                        if sn.get(k2, 0) < v2:
                            sn[k2] = v2
            inst = o.fn(**o.kw)
            self.inst_names.append((o.eng, getattr(getattr(inst, "ins", None), "name", None)))
            if o.is_dma:
                inst.then_inc(o.sem.h, 16)
            elif o.signal:
                cnt[o.eng] += 1
                o.cnt = cnt[o.eng]
                inst.then_inc(self.esem[o.eng].h, 1)
                o.snap = dict(sn)
                o.snap[self.esem[o.eng]] = o.cnt
        for s in self.st_sems:
            self.nc.sync.wait_ge(s.h, s.cnt)
        self.stats = (len(self.ops), nwait, dict(cnt))


class Buf:
    __slots__ = ("ap", "res")

    def __init__(self, ap, res):
        self.ap = ap
        self.res = res


def _bf(a):
    return np.ascontiguousarray(a.astype(ml_dtypes.bfloat16))


def make_consts():
    c = {}
    c["ident_bf"] = _bf(np.eye(128, dtype=np.float32))
    c["ident_f"] = np.eye(128, dtype=np.float32)
    c["ones_bf"] = _bf(np.ones((128, 128), np.float32))
    rm = np.zeros((128, 128), np.float32)
    for blk in range(2):
        o = blk * 64
        for i in range(16):
            rm[o + 16 + i, o + i] = -1.0
            rm[o + i, o + 16 + i] = 1.0
            rm[o + 48 + i, o + 32 + i] = -1.0
            rm[o + 32 + i, o + 48 + i] = 1.0
    c["rm"] = _bf(rm)
    t = np.arange(S_S)
    row = (t // 64).astype(np.float32)
    col = (t % 64).astype(np.float32)
    inv = (10000.0 ** (-np.arange(16, dtype=np.float32) / 16)).astype(np.float32)
    ar = row[:, None] * inv
    ac = col[:, None] * inv
    ang = np.concatenate([ar, ar, ac, ac], axis=-1)
    c["cosT"] = _bf(np.tile(np.cos(ang).T, (2, 1)))
    c["sinT"] = _bf(np.tile(np.sin(ang).T, (2, 1)))
    ka = np.arange(128)[:, None]
    qa = np.arange(128)[None, :]
    masks = np.zeros((128, 2, 128), np.float32)
    masks[:, 0, :] = np.where(ka <= qa, 1.0, 0.0)
    masks[:, 1, :] = np.where(qa <= ka, 1.0, 0.0)
    c["masks"] = _bf((1.0 - masks) * NEG)
    cc = np.arange(64)
    a = 2 * np.pi * np.outer(cc, cc) / 64.0
    ccbd = np.zeros((128, 2, 128), np.float32)
    for g in range(2):
        ccbd[g * 64:(g + 1) * 64, 0, g * 64:(g + 1) * 64] = np.cos(a) / 8.0
        ccbd[g * 64:(g + 1) * 64, 1, g * 64:(g + 1) * 64] = np.sin(a) / 8.0
    c["ccbd"] = ccbd
    def tables(S):
        s = np.arange(S, dtype=np.int64)
        m = np.outer(s, s) % S
        a = 2 * np.pi * m.astype(np.float64) / S
        return np.stack([np.cos(a), -np.sin(a)]) / math.sqrt(S)
    tp = tables(S_P)
    c["dftp"] = _bf(tp.reshape(2, 2, 128, 256).transpose(2, 0, 1, 3))
    ts = tables(S_S)
    c["dfts"] = _bf(ts.reshape(2, 16, 128, 8, 256).transpose(0, 3, 2, 1, 4))
    li = np.array([0.8 - 0.6 * math.exp(-0.3 * l) for l in range(NL)], np.float32)
    c["laminit"] = np.ascontiguousarray(np.broadcast_to(li[None, :], (128, NL))).astype(np.float32)
    c["omli"] = np.ascontiguousarray(1.0 - c["laminit"]).astype(np.float32)
    return c


CONST_SPECS = [
    ("ident_bf", [128, 128], BF16), ("ident_f", [128, 128], F32), ("ones_bf", [128, 128], BF16),
    ("rm", [128, 128], BF16), ("cosT", [128, 2048], BF16), ("sinT", [128, 2048], BF16),
    ("masks", [128, 2, 128], BF16), ("ccbd", [128, 2, 128], F32), ("dftp", [128, 2, 2, 256], BF16),
    ("laminit", [128, NL], F32), ("omli", [128, NL], F32),
]
PARAM_SPECS = [
    ("cvec", [128, 8, 2]), ("b_ada_fm", [128, NL, 24]), ("norm_g_fm", [128, NL, 8]), ("final_g_rep", [128, 1024]),
    ("lamv", [128, 4, NL, 64]), ("dgT", [128, NL]), ("sinkrep", [128, NL, 4]),
]


def build_nc(cfg):
    layers = cfg.get("layers", NL)
    do_p = cfg.get("prompt", True)
    do_s = cfg.get("sample", True)

    nc = bass.Bass("TRN2", target_bir_lowering=False)
    es = contextlib.ExitStack()

    def din(name, shape, dt=F32):
        return nc.dram_tensor(name, list(shape), dt, kind="ExternalInput").ap()

    def dout(name, shape):
        return nc.dram_tensor(name, list(shape), F32, kind="ExternalOutput").ap()

    xp = din("xp", [T_P, D])
    xs = din("xs", [T_S, D])
    cdk = din("cdk", [NL, NCTX, 512])
    cdv = din("cdv", [NL, NCTX, 512])
    csk = din("csk", [NL, NCTX, 128])
    csv = din("csv", [NL, NCTX, 128])
    w_in = din("w_in", [NL, D, IN_COLS])
    w_out = din("w_out", [NL, D, D])
    w_ada = din("w_ada", [NL, D, 3 * D])
    wfbd = din("wfbd", [NL, 256, 256])
    dfts = din("dfts", [2, 8, 128, 16, 256], BF16)
    cd = {n: din(n, s, dt) for n, s, dt in CONST_SPECS}
    pd = {n: din(n, s) for n, s in PARAM_SPECS}
    yp = dout("yp", [T_P, D])
    ys = dout("ys", [T_S, D])
    ndk = dout("ndk", [NSEQ_P, NL, S_P, 512])
    ndv = dout("ndv", [NSEQ_P, NL, S_P, 512])
    nsk = dout("nsk", [NSEQ_P, NL, S_P, 128])
    nsv = dout("nsv", [NSEQ_P, NL, S_P, 128])

    with es:
        P = Prog(nc, es)

        def sb(name, shape, dt):
            return es.enter_context(nc.sbuf_tensor(name, list(shape), dt))[:]

        xT = sb("xT", [128, 8, T_S], F32)
        hT = sb("hT", [128, 8, T_S], BF16)
        mixa = sb("mixa", [128, 4, T_S], BF16)
        mixbc = sb("mixbc", [128, 16, 512], BF16)
        wblk = [Buf(sb(f"wblk{i}", [128, 8, 512], BF16), Res(f"wblk{i}")) for i in range(2)]
        ct = {n: Buf(sb("c_" + n, s, dt), Res("c_" + n)) for n, s, dt in CONST_SPECS}
        ABm = Buf(sb("ABm", [128, 2, 2, 256], BF16), Res("ABm"))
        MOD = Buf(sb("MOD", [128, NL, 2, 3, 8], F32), Res("MOD"))
        LAM = Buf(sb("LAM", [128, NL], F32), Res("LAM"))
        DG = Buf(sb("DG", [128, NL], F32), Res("DG"))
        NLAM = Buf(sb("NLAM", [128, NL], F32), Res("NLAM"))
        sc_bf = Buf(sb("sc_bf", [128, 8, 2], BF16), Res("sc_bf"))
        modrow = Buf(sb("modrow", [2, 512], F32), Res("modrow"))
        modraw = Buf(sb("modraw", [128, 2, 24], F32), Res("modraw"))
        badat = Buf(sb("badat", [128, NL, 24], F32), Res("badat"))
        ngt = Buf(sb("ngt", [128, NL, 8], F32), Res("ngt"))
        ESK = Buf(sb("ESK", [128, NL, 4], F32), Res("ESK"))
        ARENA_N = 22784
        arena = sb("arena", [128, ARENA_N], BF16)
        psum = es.enter_context(nc.psum_tensor("ps", [128, 8, 512], F32))
        banks = [Buf(psum[:, b, :], Res(f"bank{b}", excl=True)) for b in range(8)]

        xT_res = [Res(f"xT{b}") for b in range(4)]
        hT_res = [Res(f"hT{b}") for b in range(4)]
        mix_res = [Res(f"mix{t}") for t in range(16)]
        mixa_res = [Res(f"mixa{b}") for b in range(4)]

        class Arena:
            def __init__(self):
                self.off = 0
                self.cur = []
                self.old = []

            def reset(self):
                self.old = self.old + self.cur
                self.cur = []
                self.off = 0

            def alloc(self, name, shape, dt):
                n = int(np.prod(shape[1:]))
                nel = n * (2 if dt == F32 else 1)
                pad = nel % 2
                assert self.off + nel + pad <= ARENA_N, (name, self.off, nel)
                ap = arena[:, self.off:self.off + nel]
                self.off += nel + pad
                if dt == F32:
                    ap = ap.bitcast(F32)
                if len(shape) == 3:
                    ap = ap.rearrange("p (a b) -> p a b", a=shape[1])
                elif len(shape) == 4:
                    ap = ap.rearrange("p (a b c) -> p a b c", a=shape[1], b=shape[2])
                r = Res(name, arena=True)
                self.cur.append(r)
                return Buf(ap, r)

            def commit(self):
                if self.cur:
                    P.retire(self.old, self.cur)
                    self.old = []

        ar = Arena()

        bank_rr = {"w": 0, "p": 0, "all": 0}
        bank_ids = {"w": [0, 1, 2, 3], "p": [7, 6, 0, 1, 2, 3], "all": [0, 1, 2, 3, 7, 6, 4, 5]}

        def nbank(pool):
            ids = bank_ids[pool]
            i = bank_rr[pool]
            bank_rr[pool] = i + 1
            return banks[ids[i % len(ids)]]

        wrr = [0]

        def load_w(src, ranges):
            slot = wblk[wrr[0] % 2]
            wrr[0] += 1
            sv = src.rearrange("(k p) c -> p k c", p=128)
            for (c0, n, d0) in ranges:
                P.dma("pool", slot.ap[:, :, d0:d0 + n], sv[:, :, c0:c0 + n], slot.res, "ld")
            return slot

        V = nc.vector
        A = nc.scalar
        PE = nc.tensor

        def mm(out, lhsT, rhs, start, stop, reads, writes, skip=False):
            kw = dict(out=out, lhsT=lhsT, rhs=rhs, start=start, stop=stop)
            if skip:
                kw["skip_group_check"] = True
            return P.op("pe", PE.matmul, kw, reads, writes)

        def tp(out, in_, identity, reads, writes):
            return P.op("pe", PE.transpose, dict(out=out, in_=in_, identity=identity), reads, writes)

        def dve_tt(out, in0, in1, op, reads, writes):
            return P.op("dve", V.tensor_tensor, dict(out=out, in0=in0, in1=in1, op=op), reads, writes)

        def dve_ts(out, in0, s1, s2, op0, op1, reads, writes):
            kw = dict(out=out, in0=in0, scalar1=s1, scalar2=s2, op0=op0)
            if op1 is not None:
                kw["op1"] = op1
            return P.op("dve", V.tensor_scalar, kw, reads, writes)

        def dve_stt(out, in0, scalar, in1, op0, op1, reads, writes):
            return P.op("dve", V.scalar_tensor_tensor, dict(out=out, in0=in0, scalar=scalar, in1=in1, op0=op0, op1=op1),
                        reads, writes)

        def dve_copy(out, in_, reads, writes):
            return P.op("dve", V.tensor_copy, dict(out=out, in_=in_), reads, writes)

        def dve_recip(out, in_, reads, writes):
            return P.op("dve", V.reciprocal, dict(out=out, in_=in_), reads, writes)

        def act(out, in_, func, reads, writes, scale=None, bias=None):
            kw = dict(out=out, in_=in_, func=func)
            if scale is not None:
                kw["scale"] = scale
            if bias is not None:
                kw["bias"] = bias
            return P.op("act", A.activation, kw, reads, writes)

        def act_copy(out, in_, reads, writes):
            return P.op("act", A.copy, dict(out=out, in_=in_), reads, writes)

        for n, _, _ in CONST_SPECS:
            P.dma("sp", ct[n].ap, cd[n], ct[n].res, "ld")
        ident_bf, ident_f, ones_bf, rm = ct["ident_bf"], ct["ident_f"], ct["ones_bf"], ct["rm"]

        P.dma("sp", badat.ap, pd["b_ada_fm"], badat.res, "ld")
        P.dma("sp", ngt.ap, pd["norm_g_fm"], ngt.res, "ld")
        ar.reset()
        pt = {n: ar.alloc("p_" + n, s, F32) for n, s in PARAM_SPECS if n in ("cvec", "lamv", "dgT", "sinkrep")}
        assert "final_g_rep" not in pt
        ltmp = ar.alloc("ltmp", [128, 2, NL, 64], F32)
        lsum = ar.alloc("lsum", [128, 2, NL], F32)
        sc = ar.alloc("sc", [128, 8, 2], F32)
        ar.commit()
        for n in pt:
            P.dma("sp", pt[n].ap, pd[n], pt[n].res, "ld")
        lamv = pt["lamv"]
        for i in range(2):
            dve_tt(ltmp.ap[:, i], lamv.ap[:, 2 * i], lamv.ap[:, 2 * i + 1], ALU.mult, [lamv.res], [ltmp.res])
        P.op("dve", V.reduce_sum, dict(out=lsum.ap, in_=ltmp.ap, axis=AX.X), [ltmp.res], [lsum.res])
        act(lsum.ap, lsum.ap, AF.Exp, [lsum.res], [lsum.res])
        dve_tt(LAM.ap, lsum.ap[:, 0, :], lsum.ap[:, 1, :], ALU.subtract, [lsum.res], [LAM.res])
        dve_tt(LAM.ap, LAM.ap, ct["laminit"].ap, ALU.add, [LAM.res, ct["laminit"].res], [LAM.res])
        dve_ts(NLAM.ap, LAM.ap, -1.0, None, ALU.mult, None, [LAM.res], [NLAM.res])
        dve_tt(DG.ap, pt["dgT"].ap, ct["omli"].ap, ALU.mult, [pt["dgT"].res, ct["omli"].res], [DG.res])
        act(ESK.ap, pt["sinkrep"].ap, AF.Exp, [pt["sinkrep"].res], [ESK.res])
        act(sc.ap, pt["cvec"].ap, AF.Silu, [pt["cvec"].res], [sc.res])
        dve_copy(sc_bf.ap, sc.ap, [sc.res], [sc_bf.res])

        mod_pending = [(l, blk) for l in range(layers) for blk in range(6)]

        def mod_step():
            if not mod_pending:
                return
            l, blk = mod_pending.pop(0)
            slot = load_w(w_ada[l], [(blk * 512, 512, 0)])
            bk = nbank("all")
            for k in range(KD):
                mm(bk.ap[0:2, :], sc_bf.ap[:, k, :], slot.ap[:, k, :], k == 0, k == KD - 1, [sc_bf.res, slot.res], [bk.res])
            dve_copy(modrow.ap, bk.ap[0:2, :], [bk.res], [modrow.res])
            bk2 = nbank("all")
            for jl in range(4):
                tp(bk2.ap[:, 2 * jl:2 * jl + 2], modrow.ap[0:2, jl * 128:(jl + 1) * 128], ident_f.ap[0:2, 0:2],
                   [modrow.res, ident_f.res], [bk2.res])
            dve_copy(modraw.ap[:, :, 4 * blk:4 * blk + 4], bk2.ap[:, 0:8].rearrange("p (j v) -> p v j", v=2),
                     [bk2.res], [modraw.res])
            if blk == 5:
                for v in range(2):
                    dve_tt(modraw.ap[:, v, :], modraw.ap[:, v, :], badat.ap[:, l, :], ALU.add,
                           [modraw.res, badat.res], [modraw.res])
                    dve_stt(MOD.ap[:, l, v, 0, :], modraw.ap[:, v, 8:16], 1.0, ngt.ap[:, l, :], ALU.add, ALU.mult,
                            [modraw.res, ngt.res], [MOD.res])
                    dve_copy(MOD.ap[:, l, v, 1, :], modraw.ap[:, v, 0:8], [modraw.res], [MOD.res])
                    dve_copy(MOD.ap[:, l, v, 2, :], modraw.ap[:, v, 16:24], [modraw.res], [MOD.res])

        for _ in range(6):
            mod_step()

        def load_x(xd, T):
            ar.reset()
            xst = [ar.alloc(f"xst{i}", [128, 1024], F32) for i in range(2)]
            ar.commit()
            for tt in range(T // 128):
                st = xst[tt % 2]
                P.dma("sp", st.ap, xd[tt * 128:(tt + 1) * 128, :], st.res, "ld")
                for half in range(2):
                    bk = nbank("all")
                    for j in range(4):
                        k = half * 4 + j
                        tp(bk.ap[:, j * 128:(j + 1) * 128], st.ap[:, k * 128:(k + 1) * 128], ident_f.ap,
                           [st.res, ident_f.res], [bk.res])
                    dst = xT[:, half * 4:half * 4 + 4, tt * 128:(tt + 1) * 128]
                    src = bk.ap.rearrange("p (a b) -> p a b", a=4)
                    if half == 0:
                        dve_copy(dst, src, [bk.res], [xT_res[tt // 4]])
                    else:
                        act_copy(dst, src, [bk.res], [xT_res[tt // 4]])

        def norm_stage(T, g_ap, sh_ap, gres, final_out=None):
            ar.reset()
            nb_ = 2 if final_out is None else 1
            sqts = [ar.alloc(f"sqt{i}", [128, 8, 512], BF16) for i in range(nb_)]
            rstds = [ar.alloc(f"rstd{i}", [128, 512], F32) for i in range(nb_)]
            ntf = 4 if final_out is None else 2
            tmpf = [ar.alloc(f"tmpf{i}", [128, 512], F32) for i in range(ntf)]
            if final_out is not None:
                t2 = ar.alloc("t2", [128, 8, 512], F32)
                ost = [ar.alloc(f"ost{i}", [128, 1024], F32) for i in range(2)]
            ar.commit()
            nblk_ = T // 512
            bks = {}

            def stats(b):
                blk = slice(b * 512, (b + 1) * 512)
                sqt = sqts[b % nb_]
                act(sqt.ap, xT[:, :, blk], AF.Square, [xT_res[b]], [sqt.res])
                bk = nbank("all")
                for k in range(KD):
                    mm(bk.ap, ones_bf.ap, sqt.ap[:, k, :], k == 0, k == KD - 1, [ones_bf.res, sqt.res], [bk.res])
                bks[b] = bk

            stats(0)
            for b in range(nblk_):
                blk = slice(b * 512, (b + 1) * 512)
                rstd = rstds[b % nb_]
                if b + 1 < nblk_ and nb_ == 2:
                    stats(b + 1)
                bk = bks[b]
                act(rstd.ap, bk.ap, AF.Ln, [bk.res], [rstd.res], scale=1.0 / D, bias=EPS)
                act(rstd.ap, rstd.ap, AF.Exp, [rstd.res], [rstd.res], scale=-0.5)
                for k in range(KD):
                    tf = tmpf[k % ntf]
                    if final_out is None and k in (2, 6):
                        P.op("pool", nc.gpsimd.tensor_tensor, dict(out=tf.ap, in0=xT[:, k, blk], in1=rstd.ap, op=ALU.mult),
                             [xT_res[b], rstd.res], [tf.res])
                    else:
                        dve_tt(tf.ap, xT[:, k, blk], rstd.ap, ALU.mult, [xT_res[b], rstd.res], [tf.res])
                    if final_out is None and k % 2 == 1:
                        dve_ts(hT[:, k, blk], tf.ap, g_ap[:, k:k + 1], sh_ap[:, k:k + 1], ALU.mult, ALU.add,
                               [tf.res, gres], [hT_res[b]])
                    elif final_out is None:
                        act(hT[:, k, blk], tf.ap, AF.Identity, [tf.res, gres], [hT_res[b]],
                            scale=g_ap[:, k:k + 1], bias=sh_ap[:, k:k + 1])
                    else:
                        act(t2.ap[:, k, :], tf.ap, AF.Identity, [tf.res, gres], [t2.res], scale=g_ap[:, k:k + 1])
                if final_out is not None:
                    for tq in range(4):
                        tt = b * 4 + tq
                        os_ = ost[tt % 2]
                        for half in range(2):
                            bk2 = nbank("all")
                            for j in range(4):
                                k = half * 4 + j
                                tp(bk2.ap[:, j * 128:(j + 1) * 128], t2.ap[:, k, tq * 128:(tq + 1) * 128], ident_f.ap,
                                   [t2.res, ident_f.res], [bk2.res])
                            if half == 0:
                                dve_copy(os_.ap[:, 0:512], bk2.ap, [bk2.res], [os_.res])
                            else:
                                act_copy(os_.ap[:, 512:1024], bk2.ap, [bk2.res], [os_.res])
                        P.dma("sp", final_out[tt * 128:(tt + 1) * 128, :], os_.ap, os_.res, "st")
                if b + 1 < nblk_ and nb_ == 1:
                    stats(b + 1)

        def final_stage(T, y_out):
            ar.reset()
            gfull = ar.alloc("gfull", [128, 1024], F32)
            ost = [ar.alloc(f"ost{i}", [128, 1024], F32) for i in range(3)]
            junk = ar.alloc("junk", [128, 512], BF16)
            st4 = [ar.alloc(f"st4_{i}", [128, 4], F32) for i in range(3)]
            ar.commit()
            P.dma("sp", gfull.ap, pd["final_g_rep"], gfull.res, "ld")
            for tt in range(T // 128):
                os_, s4 = ost[tt % 3], st4[tt % 3]
                bkh = []
                for half in range(2):
                    bk = nbank("all")
                    for j in range(4):
                        k = half * 4 + j
                        tp(bk.ap[:, j * 128:(j + 1) * 128], xT[:, k, tt * 128:(tt + 1) * 128], ident_f.ap,
                           [xT_res[tt // 4], ident_f.res], [bk.res])
                    bkh.append(bk)
                    P.op("act", A.activation, dict(out=junk.ap, in_=bk.ap, func=AF.Square, accum_out=s4.ap[:, half:half + 1]),
                         [bk.res], [junk.res, s4.res])
                dve_tt(s4.ap[:, 2:3], s4.ap[:, 0:1], s4.ap[:, 1:2], ALU.add, [s4.res], [s4.res])
                act(s4.ap[:, 3:4], s4.ap[:, 2:3], AF.Ln, [s4.res], [s4.res], scale=1.0 / D, bias=EPS)
                act(s4.ap[:, 3:4], s4.ap[:, 3:4], AF.Exp, [s4.res], [s4.res], scale=-0.5)
                for half in range(2):
                    dve_stt(os_.ap[:, half * 512:(half + 1) * 512], bkh[half].ap, s4.ap[:, 3:4],
                            gfull.ap[:, half * 512:(half + 1) * 512], ALU.mult, ALU.mult,
                            [bkh[half].res, s4.res, gfull.res], [os_.res])
                P.dma("sp", y_out[tt * 128:(tt + 1) * 128, :], os_.ap, os_.res, "st")

        def proj_fm(slot, c0, b, n=512):
            bk = nbank("p")
            for k in range(KD):
                mm(bk.ap[:, 0:n], slot.ap[:, k, c0:c0 + 128], hT[:, k, b * 512:b * 512 + n], k == 0, k == KD - 1,
                   [slot.res, hT_res[b]], [bk.res])
            return bk

        def proj_tm(slot, c0, n, tt):
            bk = nbank("p")
            for k in range(KD):
                mm(bk.ap[:, 0:n], hT[:, k, tt * 128:(tt + 1) * 128], slot.ap[:, k, c0:c0 + n], k == 0, k == KD - 1,
                   [slot.res, hT_res[tt // 4]], [bk.res])
            return bk

        def proj_rope_pipelined(items, rp):
            bks = [None] * len(items)
            if items:
                bks[0] = proj_fm(items[0][0], items[0][1], items[0][2])
            for i, (slot_, c0, b, dst_ap, dst_res, tok0) in enumerate(items):
                if i + 1 < len(items):
                    bks[i + 1] = proj_fm(items[i + 1][0], items[i + 1][1], items[i + 1][2])
                rope_or_copy(bks[i], dst_ap, dst_res, tok0, 512, rp)

        def rope_or_copy(bk, dst_ap, dst_res, tok0, n, rp):
            if rp is None:
                act_copy(dst_ap, bk.ap[:, 0:n], [bk.res], [dst_res])
                return
            qpre, t1, t2, kk = rp
            qp = qpre[kk[0] % 2]
            kk[0] += 1
            act_copy(qp.ap[:, 0:n], bk.ap[:, 0:n], [bk.res], [qp.res])
            b2 = nbank("p")
            mm(b2.ap[:, 0:n], rm.ap, qp.ap[:, 0:n], True, True, [rm.res, qp.res], [b2.res])
            dve_tt(t1.ap[:, 0:n], b2.ap[:, 0:n], ct["sinT"].ap[:, tok0:tok0 + n], ALU.mult, [b2.res, ct["sinT"].res], [t1.res])
            dve_tt(t2.ap[:, 0:n], qp.ap[:, 0:n], ct["cosT"].ap[:, tok0:tok0 + n], ALU.mult, [qp.res, ct["cosT"].res], [t2.res])
            dve_tt(dst_ap, t1.ap[:, 0:n], t2.ap[:, 0:n], ALU.add, [t1.res, t2.res], [dst_res])

        STP = [(banks[0], banks[1]), (banks[2], banks[3])]
        STP3 = STP + [(banks[6], banks[7])]

        def run_attention(jobs, ET, nq):
            groups = []
            for ji, job in enumerate(jobs):
                job["started"] = set()
                job["accset"] = ji % 2
                gl = job["groups"]
                for gi, sts in enumerate(gl):
                    assert len(set(st[8] for st in sts)) == 1
                    groups.append((job, sts, gi == len(gl) - 1))
            n = len(groups)

            def acc_ap(job, slot):
                dv1 = job["dv1"]
                per = 512 // dv1
                bkb = banks[4 + job["accset"] + 2 * (slot // per)]
                c0 = (slot % per) * dv1
                return bkb, bkb.ap[:, c0:c0 + dv1]

            def QK(g):
                job, sts, last = groups[g]
                for si, (Qap, Kap, Kres, mask, Vap, Vres, outs, part, N) in enumerate(sts):
                    bk = STP3[g % 3][si]
                    mm(bk.ap[:, 0:N], Kap, Qap, True, True, [Kres, job["qres"]], [bk.res])
                    for pi, (i_, which) in enumerate(part):
                        mm(bk.ap[:, i_ * 128:(i_ + 1) * 128], ident_bf.ap, ct["masks"].ap[:, which, :], False,
                           pi == len(part) - 1, [ident_bf.res, ct["masks"].res], [bk.res], skip=True)

            def EXP(g):
                job, sts, last = groups[g]
                et = ET[g % 3]
                ns = len(sts)
                N = sts[0][8]
                b0 = (0, 2, 6)[g % 3]
                act(et.ap[:, 0:ns, 0:N], psum[:, b0:b0 + ns, 0:N], AF.Exp,
                    [STP3[g % 3][si].res for si in range(ns)], [et.res], scale=SCALE)

            def PV(g):
                job, sts, last = groups[g]
                et = ET[g % 3]
                for si, (Qap, Kap, Kres, mask, Vap, Vres, outs, part, N) in enumerate(sts):
                    for (slot, qt) in outs:
                        bkb, oap = acc_ap(job, slot)
                        first = id(bkb) not in job["started"]
                        job["started"].add(id(bkb))
                        mm(oap, et.ap[:, si, qt * 128:(qt + 1) * 128], Vap, first, last and si == len(sts) - 1,
                           [et.res, Vres], [bkb.res], skip=True)
                if last:
                    job["post"](job, lambda slot: acc_ap(job, slot))

            for g in range(min(2, n)):
                QK(g)
            for g in range(n):
                EXP(g)
                if g + 2 < n:
                    QK(g + 2)
                PV(g)

        def layer(l, T, S, nseq, sample, path, xout_k, xout_v, xout_sk, xout_sv):
            ntile = T // 128
            nblk = T // 512
            nst = S // 128
            nkt = nst + (2 if sample else 0)
            nq = min(512, S)
            nqt = nq // 128
            g_ap = MOD.ap[:, l, path, 0, :]
            sh_ap = MOD.ap[:, l, path, 1, :]
            gate_ap = MOD.ap[:, l, path, 2, :]

            P.mark(f"L{l}{'s' if sample else 'p'}:norm")
            norm_stage(T, g_ap, sh_ap, MOD.res)
            P.mark(f"L{l}{'s' if sample else 'p'}:fourier")

            if not sample:
                mod_step()
            ar.reset()
            u = [ar.alloc(f"u{t}", [128, 256], BF16) for t in range(ntile)]
            zsc = [ar.alloc(f"zsc{t}", [128, 256], BF16) for t in range(ntile)]
            YT = [ar.alloc(f"YT{i}", [128, 2, 2, 256], BF16) for i in range(2)]
            wfst = ar.alloc("wfst", [128, 2, 256], F32)
            if sample:
                ring = [ar.alloc(f"dft{i}", [128, 16, 256], BF16) for i in range(2)]
            ar.commit()
            P.dma("sp", wfst.ap, wfbd[l].rearrange("(j p) c -> p j c", p=128), wfst.res, "ld")
            for j in range(2):
                bk = nbank("all")
                for cs in range(2):
                    mm(bk.ap[:, cs * 256:(cs + 1) * 256], ct["ccbd"].ap[:, cs, :], wfst.ap[:, j, :], True, True,
                       [ct["ccbd"].res, wfst.res], [bk.res])
                act_copy(ABm.ap[:, j, :, :], bk.ap.rearrange("p (a b) -> p a b", a=2), [bk.res], [ABm.res])
            slot = load_w(w_in[l], [(UC0, 256, 0), (Z0 + 768, 256, 256)])
            for tt in range(ntile):
                bk = proj_tm(slot, 0, 512, tt)
                dve_copy(u[tt].ap, bk.ap[:, 0:256], [bk.res], [u[tt].res])
                act(zsc[tt].ap, bk.ap[:, 256:512], AF.Silu, [bk.res], [zsc[tt].res])
            rr = 0
            for seq in range(nseq):
                for sbk in range(S // 256):
                    yt = YT[(seq * (S // 256) + sbk) % 2]
                    for cs in range(2):
                        if sample:
                            tb = ring[rr % 2]
                            rr += 1
                            P.dma("sp", tb.ap, dfts[cs, sbk], tb.res, "ld")
                            tb_ap, tb_res = tb.ap, tb.res
                        else:
                            tb_ap, tb_res = ct["dftp"].ap[:, cs], ct["dftp"].res
                        bk = nbank("w")
                        for j in range(2):
                            for st in range(nst):
                                ut = u[seq * nst + st]
                                mm(bk.ap[:, j * 256:(j + 1) * 256], ut.ap[:, j * 128:(j + 1) * 128], tb_ap[:, st, :],
                                   st == 0, st == nst - 1, [ut.res, tb_res], [bk.res])
                        src = bk.ap.rearrange("p (a b) -> p a b", a=2)
                        if cs == 0:
                            act_copy(yt.ap[:, 0, :, :], src, [bk.res], [yt.res])
                        else:
                            dve_copy(yt.ap[:, 1, :, :], src, [bk.res], [yt.res])
                    for t2 in range(2):
                        tt = seq * nst + sbk * 2 + t2
                        bk = nbank("p")
                        i = 0
                        for cs in range(2):
                            for j in range(2):
                                mm(bk.ap[:, 0:256], yt.ap[:, cs, j, t2 * 128:(t2 + 1) * 128], ABm.ap[:, j, cs, :],
                                   i == 0, i == 3, [yt.res, ABm.res], [bk.res])
                                i += 1
                        dve_tt(mixbc[:, tt, 256:512], bk.ap[:, 0:256], zsc[tt].ap, ALU.mult, [bk.res, zsc[tt].res],
                               [mix_res[tt]])

            P.mark(f"L{l}{'s' if sample else 'p'}:swa")
            ar.reset()
            QTB = [ar.alloc(f"QTB{g}", [128, T], BF16) for g in range(2)]
            KTB = ar.alloc("KTB", [128, nseq * nkt * 128], BF16)
            V1B = ar.alloc("V1B", [128, nseq * nkt, 2, 65], BF16)
            zsb = [ar.alloc(f"zsb{t}", [128, 256], BF16) for t in range(ntile)]
            ET = [ar.alloc(f"ET{i}", [128, 2, 512], BF16) for i in range(3)]
            ptmp = ar.alloc("ptmp", [128, 8], F32)
            if sample:
                rp = ([ar.alloc(f"qpre{i}", [128, 512], BF16) for i in range(2)], ar.alloc("t1", [128, 512], F32),
                      ar.alloc("t2", [128, 512], F32), [0])
                cst = ar.alloc("cst", [128, 2, 128], F32)
            else:
                rp = None
                stg = [ar.alloc(f"stg{i}", [128, 256], F32) for i in range(3)]
            ar.commit()
            slot = load_w(w_in[l], [(BQ0, 64, 0), (BQ0 + 128, 64, 64), (BQ0 + 64, 64, 128), (BQ0 + 192, 64, 192),
                                    (BK0, 128, 256), (BV0, 128, 384)])
            slot2 = load_w(w_in[l], [(Z0 + 512, 256, 0)])
            P.op("dve", V.memset, dict(ap=V1B.ap[:, :, :, 64:65], constant=1.0), [], [V1B.res])
            items = []
            for b in range(nblk):
                for g in range(2):
                    items.append((slot, g * 128, b, QTB[g].ap[:, b * 512:(b + 1) * 512], QTB[g].res, b * 512))
                items.append((slot, 256, b, KTB.ap[:, b * 512:(b + 1) * 512], KTB.res, b * 512))
            proj_rope_pipelined(items, rp)
            for tt in range(ntile):
                bk = proj_tm(slot, 256, 256, tt)
                act_copy(V1B.ap[:, tt, :, 0:64], bk.ap[:, 128:256].rearrange("p (a b) -> p a b", a=2), [bk.res], [V1B.res])
                if not sample:
                    sg = stg[tt % 3]
                    seq, st = tt // nst, tt % nst
                    dve_copy(sg.ap, bk.ap[:, 0:256], [bk.res], [sg.res])
                    P.dma("sp", xout_sk[seq, l, st * 128:(st + 1) * 128, :], sg.ap[:, 0:128], sg.res, "st")
                    P.dma("sp", xout_sv[seq, l, st * 128:(st + 1) * 128, :], sg.ap[:, 128:256], sg.res, "st")
                bk = proj_tm(slot2, 0, 256, tt)
                act(zsb[tt].ap, bk.ap[:, 0:256], AF.Silu, [bk.res], [zsb[tt].res])
            if sample:
                P.dma("sp", cst.ap, csk[l].rearrange("(t p) c -> p t c", p=128), cst.res, "ld")
                bk = nbank("p")
                for t in range(2):
                    tp(bk.ap[:, t * 128:(t + 1) * 128], cst.ap[:, t, :], ident_f.ap, [cst.res, ident_f.res], [bk.res])
                act_copy(KTB.ap[:, S:S + 256], bk.ap[:, 0:256], [bk.res], [KTB.res])
                for t in range(2):
                    P.dma("pool", V1B.ap[:, nst + t, :, 0:64],
                          csv[l, t * 128:(t + 1) * 128, :].rearrange("p (a b) -> p a b", a=2), V1B.res, "ld")

            def make_post_b(hb, ptmp=ptmp, zsb=zsb):
                def post(job, acc):
                    nq_ = job["nqt"]
                    bkb, a0 = acc(0)
                    sums = bkb.ap[:, 0:nq_ * 65].rearrange("p (s c) -> p s c", c=65)[:, :, 64]
                    dve_ts(ptmp.ap[:, 0:nq_], sums, ESK.ap[:, l, hb:hb + 1], None, ALU.add, None,
                           [bkb.res, ESK.res], [ptmp.res])
                    dve_recip(ptmp.ap[:, 4:4 + nq_], ptmp.ap[:, 0:nq_], [ptmp.res], [ptmp.res])
                    for qt in range(nq_):
                        tt = job["tt0"] + qt
                        bkb, a_ = acc(qt)
                        dve_stt(mixbc[:, tt, hb * 64:(hb + 1) * 64], a_[:, 0:64], ptmp.ap[:, 4 + qt:5 + qt],
                                zsb[tt].ap[:, hb * 64:(hb + 1) * 64], ALU.mult, ALU.mult,
                                [bkb.res, ptmp.res, zsb[tt].res], [mix_res[tt]])
                return post

            jobs = []
            for seq in range(nseq):
                for hb in range(4):
                    g, kvh = hb % 2, hb // 2
                    rows = slice(kvh * 64, (kvh + 1) * 64)
                    for qb in range(S // nq):
                        q0 = seq * S + qb * nq

                        def kstep(j, lo, nt, masked, n0=0):
                            outs = [(lo + i, i) for i in range(nt)]
                            part = []
                            if masked:
                                part = [(i, 0 if n0 + lo + i == j - 1 else 1) for i in range(nt) if n0 + lo + i != j]
                            return (QTB[g].ap[rows, q0 + lo * 128:q0 + (lo + nt) * 128],
                                    KTB.ap[rows, j * 128:(j + 1) * 128], KTB.res, masked,
                                    V1B.ap[:, j, kvh, :], V1B.res, outs, part, nt * 128)

                        if sample:
                            n0 = qb * 4
                            gl = []
                            for ja, jb in ((n0 - 1, n0 + 4), (n0, n0 + 3), (n0 + 1, n0 + 2)):
                                grp = []
                                for j in (ja, jb):
                                    if 0 <= j < nst:
                                        lo = max(j - 1, n0) - n0
                                        hi = min(j + 1, n0 + 3) - n0
                                        grp.append(kstep(j, lo, hi - lo + 1, True, n0))
                                if grp:
                                    gl.append(grp)
                            gl.append([kstep(nst + t, 0, 4, False) for t in range(2)])
                        else:
                            gl = [[kstep(seq * nst + st, 0, nqt, False) for st in range(nst)]]
                        jobs.append(dict(qres=QTB[g].res, groups=gl, dv1=65, post=make_post_b(hb), nqt=nqt, tt0=q0 // 128))
            P.mark(f"L{l}{'s' if sample else 'p'}:swa_attn")
            run_attention(jobs, ET, nq)

            ar.reset()
            QT = ar.alloc("QT", [128, T], BF16)
            KT = ar.alloc("KT", [128, nseq * nkt * 128], BF16)
            VA = ar.alloc("VA", [128, nseq * nkt, 128], BF16)
            zsTs = [ar.alloc(f"zsT{i}", [128, T], BF16) for i in range(2)]
            ET = [ar.alloc(f"ET{i}", [128, 2, 512], BF16) for i in range(3)]

            r1 = ar.alloc("r1", [128, 512], F32)
            r2 = ar.alloc("r2", [128, 512], F32)
            la = ar.alloc("la", [128, 512], F32)
            lb = ar.alloc("lb", [128, 512], F32)
            sq = ar.alloc("sq", [128, 512], BF16)
            if sample:
                rp = ([ar.alloc(f"qpre{i}", [128, 512], BF16) for i in range(2)], ar.alloc("t1", [128, 512], F32),
                      ar.alloc("t2", [128, 512], F32), [0])
                cst = ar.alloc("cst", [128, 2, 128], F32)
            else:
                rp = None
                stg = [ar.alloc(f"stg{i}", [128, 256], F32) for i in range(3)]
            ar.commit()
            queue = []
            held = []
            for h in range(4):
                zsT = zsTs[h % 2]
                P.mark(f"L{l}{'s' if sample else 'p'}:diff{h}")
                if not sample:
                    mod_step()
                slot = load_w(w_in[l], [(AQ0 + h * 128, 128, 0), (AK0 + h * 128, 128, 128), (AV0 + h * 128, 128, 256),
                                        (Z0 + h * 128, 128, 384)])
                items = []
                for b in range(nblk):
                    blk = slice(b * 512, (b + 1) * 512)
                    items.append((slot, 0, b, QT.ap[:, blk], QT.res, b * 512))
                    items.append((slot, 128, b, KT.ap[:, blk], KT.res, b * 512))
                proj_rope_pipelined(items, rp)
                for tt in range(ntile):
                    if sample:
                        bk = proj_tm(slot, 256, 128, tt)
                        vo = 0
                    else:
                        bk = proj_tm(slot, 128, 256, tt)
                        vo = 128
                    act_copy(VA.ap[:, tt, :], bk.ap[:, vo:vo + 128], [bk.res], [VA.res])
                    if not sample:
                        sg = stg[tt % 3]
                        seq, st = tt // nst, tt % nst
                        dve_copy(sg.ap, bk.ap[:, 0:256], [bk.res], [sg.res])
                        P.dma("sp", xout_k[seq, l, st * 128:(st + 1) * 128, h * 128:(h + 1) * 128], sg.ap[:, 0:128],
                              sg.res, "st")
                        P.dma("sp", xout_v[seq, l, st * 128:(st + 1) * 128, h * 128:(h + 1) * 128], sg.ap[:, 128:256],
                              sg.res, "st")
                for b in range(nblk):
                    blk = slice(b * 512, (b + 1) * 512)
                    bk = proj_fm(slot, 384, b)
                    act(zsT.ap[:, blk], bk.ap, AF.Silu, [bk.res], [zsT.res])
                if sample:
                    P.dma("sp", cst.ap, cdk[l, :, h * 128:(h + 1) * 128].rearrange("(t p) c -> p t c", p=128), cst.res, "ld")
                    bk = nbank("p")
                    for t in range(2):
                        tp(bk.ap[:, t * 128:(t + 1) * 128], cst.ap[:, t, :], ident_f.ap, [cst.res, ident_f.res], [bk.res])
                    act_copy(KT.ap[:, S:S + 256], bk.ap[:, 0:256], [bk.res], [KT.res])
                    for t in range(2):
                        P.dma("pool", VA.ap[:, nst + t, :], cdv[l, t * 128:(t + 1) * 128, h * 128:(h + 1) * 128],
                              VA.res, "ld")

                P.mark(f"L{l}{'s' if sample else 'p'}:diff{h}attn")
                OT = [banks[4], banks[5]]
                SM = [banks[6], banks[7]]
                flat = []
                if sample:
                    for qb in range(S // nq):
                        for kt in range(nkt):
                            flat.append((qb * nq, 0, qb * nq, kt, kt, kt == 0, kt == nkt - 1))
                else:
                    for sp in range(nseq // 2):
                        for s2 in range(2):
                            seq = 2 * sp + s2
                            for kt in range(nkt):
                                flat.append((sp * 512, s2 * 256, seq * S, kt, seq * nkt + kt,
                                             s2 == 0 and kt == 0, s2 == 1 and kt == nkt - 1))
                n = len(flat)
                NQJ = 512

                def QK(g):
                    qj, co, q0, kt, j, fj, lj = flat[g]
                    for c in range(2):
                        rows = slice(c * 64, (c + 1) * 64)
                        bk = STP[g % 2][c]
                        mm(bk.ap[:, 0:nq], KT.ap[rows, j * 128:(j + 1) * 128], QT.ap[rows, q0:q0 + nq], True, True,
                           [KT.res, QT.res], [bk.res])

                def ep_free(q0):
                    dve_copy(r1.ap[:, 0:NQJ], OT[0].ap[:, 0:NQJ], [OT[0].res], [r1.res])
                    dve_copy(r2.ap[:, 0:NQJ], OT[1].ap[:, 0:NQJ], [OT[1].res], [r2.res])
                    dve_copy(la.ap[:, 0:NQJ], SM[0].ap[:, 0:NQJ], [SM[0].res], [la.res])
                    dve_copy(lb.ap[:, 0:NQJ], SM[1].ap[:, 0:NQJ], [SM[1].res], [lb.res])

                def ep_ln(buf):
                    return lambda q0, bk: act(buf.ap[:, 0:NQJ], buf.ap[:, 0:NQJ], AF.Ln, [buf.res], [buf.res])

                def ep_rcp(buf):
                    return lambda q0, bk: act(buf.ap[:, 0:NQJ], buf.ap[:, 0:NQJ], AF.Exp, [buf.res], [buf.res], scale=-1.0)

                def ep_comb(q0, bk):
                    dve_tt(r1.ap[:, 0:NQJ], r1.ap[:, 0:NQJ], la.ap[:, 0:NQJ], ALU.mult, [r1.res, la.res], [r1.res])
                    dve_tt(r2.ap[:, 0:NQJ], r2.ap[:, 0:NQJ], lb.ap[:, 0:NQJ], ALU.mult, [r2.res, lb.res], [r2.res])
                    dve_stt(r1.ap[:, 0:NQJ], r2.ap[:, 0:NQJ], NLAM.ap[:, l:l + 1], r1.ap[:, 0:NQJ], ALU.mult, ALU.add,
                            [r2.res, NLAM.res, r1.res], [r1.res])
                    dve_tt(sq.ap[:, 0:NQJ], r1.ap[:, 0:NQJ], r1.ap[:, 0:NQJ], ALU.mult, [r1.res], [sq.res])

                def ep_ssq(q0, bk):
                    mm(bk.ap[:, 0:NQJ], ones_bf.ap, sq.ap[:, 0:NQJ], True, True, [ones_bf.res, sq.res], [bk.res])
                    act(la.ap[:, 0:NQJ], bk.ap[:, 0:NQJ], AF.Ln, [bk.res], [la.res], scale=1.0 / 128, bias=EPS)

                def ep_fin(q0, bk, h=h, zsT=zsT):
                    act(la.ap[:, 0:NQJ], la.ap[:, 0:NQJ], AF.Exp, [la.res], [la.res], scale=-0.5)
                    dve_tt(r2.ap[:, 0:NQJ], r1.ap[:, 0:NQJ], la.ap[:, 0:NQJ], ALU.mult, [r1.res, la.res], [r2.res])
                    dve_stt(mixa[:, h, q0:q0 + NQJ], r2.ap[:, 0:NQJ], DG.ap[:, l:l + 1], zsT.ap[:, q0:q0 + NQJ],
                            ALU.mult, ALU.mult, [r2.res, DG.res, zsT.res], [mixa_res[q0 // 512]])

                stages = [ep_ln(la), ep_ln(lb), ep_rcp(la), ep_rcp(lb), ep_comb, ep_ssq, ep_fin]
                spread = nkt >= len(stages) + 2
                if n:
                    QK(0)
                for g in range(n):
                    qj, co, q0, kt, j, fj, lj = flat[g]
                    et = ET[g % 3]
                    b0_ = 2 * (g % 2)
                    act(et.ap[:, :, 0:nq], psum[:, b0_:b0_ + 2, 0:nq], AF.Exp, [STP[g % 2][0].res, STP[g % 2][1].res],
                        [et.res], scale=SCALE)
                    if queue:
                        fn, qq = queue.pop(0)
                        fn(qq, STP[g % 2][0] if fn is not ep_ssq else None) if fn is not ep_ssq else queue.insert(0, (fn, qq))
                    if g + 1 < n:
                        QK(g + 1)
                    for c in range(2):
                        mm(OT[c].ap[:, co:co + nq], VA.ap[:, j, :], et.ap[:, c, 0:nq], fj, lj,
                           [VA.res, et.res], [OT[c].res], skip=True)
                    for c in range(2):
                        mm(SM[c].ap[:, co:co + nq], ones_bf.ap, et.ap[:, c, 0:nq], fj, lj,
                           [ones_bf.res, et.res], [SM[c].res], skip=True)
                    if queue and queue[0][0] is ep_ssq:
                        fn, qq = queue.pop(0)
                        fn(qq, STP[g % 2][0])
                    if lj:
                        if not spread and held:
                            for fn in held:
                                fn(STP[g % 2][0])
                            held.clear()
                        ep_free(qj)
                        if spread:
                            queue.extend((fn, qj) for fn in stages)
                        else:
                            held.extend((lambda bk, fn=fn, qj=qj: fn(qj, bk)) for fn in stages)

            while queue:
                fn, qq = queue.pop(0)
                fn(qq, banks[0])
            for fn in held:
                fn(banks[0])
            held.clear()

            P.mark(f"L{l}{'s' if sample else 'p'}:outproj")
            if not sample:
                mod_step()
            for tt in range(ntile):
                bk = nbank("all")
                bkb = bk.ap.bitcast(BF16)
                for j in range(4):
                    tp(bkb[:, j * 128:(j + 1) * 128], mixbc[:, tt, j * 128:(j + 1) * 128], ident_bf.ap,
                       [mix_res[tt], ident_bf.res], [bk.res])
                dst = hT[:, 4:8, tt * 128:(tt + 1) * 128]
                srcv = bkb[:, 0:512].rearrange("p (a b) -> p a b", a=4)
                if tt % 2 == 0:
                    dve_copy(dst, srcv, [bk.res], [hT_res[tt // 4]])
                else:
                    act_copy(dst, srcv, [bk.res], [hT_res[tt // 4]])
            for cb in range(2):
                slot = load_w(w_out[l], [(cb * 512, 512, 0)])
                for dc in range(4):
                    d = cb * 4 + dc
                    for b in range(nblk):
                        blk = slice(b * 512, (b + 1) * 512)
                        bk = nbank("all")
                        for f in range(KD):
                            if f < 4:
                                rhs, rres = mixa[:, f, blk], mixa_res[b]
                            else:
                                rhs, rres = hT[:, f, blk], hT_res[b]
                            mm(bk.ap, slot.ap[:, f, dc * 128:(dc + 1) * 128], rhs, f == 0, f == KD - 1,
                               [slot.res, rres], [bk.res])
                        dve_stt(xT[:, d, blk], bk.ap, gate_ap[:, d:d + 1], xT[:, d, blk], ALU.mult, ALU.add,
                                [bk.res, MOD.res, xT_res[b]], [xT_res[b]])

        if do_p:
            load_x(xp, T_P)
            for l in range(layers):
                layer(l, T_P, S_P, NSEQ_P, False, 0, ndk, ndv, nsk, nsv)
        while mod_pending:
            mod_step()
        P.mark("final_p")
        if do_p:
            final_stage(T_P, yp)
        P.mark("sample_start")
        if do_s:
            load_x(xs, T_S)
            for l in range(layers):
                layer(l, T_S, S_S, 1, True, 1, None, None, None, None)
            final_stage(T_S, ys)

        P.emit()
        build_nc.stats = P.stats
        build_nc.marks = [(lab, idx, P.inst_names[idx:idx + 40]) for lab, idx in P.marks if idx < len(P.inst_names)]
    return nc


_CONSTS = None


def make_in_maps(x_prompt, x_sample, cache_diff_k, cache_diff_v, cache_swa_k, cache_swa_v, c, c_ctx,
                 w_ada, b_ada, norm_g, w_in, lam_q1, lam_k1, lam_q2, lam_k2, diff_norm_g, sink,
                 w_fourier, w_out, final_g):
    global _CONSTS
    f = lambda a: np.ascontiguousarray(np.asarray(a, dtype=np.float32))
    x_prompt, x_sample = f(x_prompt), f(x_sample)
    if _CONSTS is None:
        _CONSTS = make_consts()
    consts = _CONSTS

    def fm(v, n):
        return np.ascontiguousarray(f(v).reshape(n, 128).T)

    shared = {
        "w_in": f(w_in), "w_out": f(w_out), "w_ada": f(w_ada),
        "b_ada_fm": np.ascontiguousarray(f(b_ada).reshape(NL, 24, 128).transpose(2, 0, 1)),
        "norm_g_fm": np.ascontiguousarray(f(norm_g).reshape(NL, 8, 128).transpose(2, 0, 1)),
        "final_g_rep": np.ascontiguousarray(np.broadcast_to(f(final_g)[None, :], (128, D))),
        "lamv": np.ascontiguousarray(np.broadcast_to(
            np.stack([f(lam_q1), f(lam_k1), f(lam_q2), f(lam_k2)])[None], (128, 4, NL, 64))),
        "dgT": np.ascontiguousarray(f(diff_norm_g).T),
        "sinkrep": np.ascontiguousarray(np.broadcast_to(f(sink).reshape(NL, 4)[None], (128, NL, 4))),
        "dfts": consts["dfts"],
    }
    wf = f(w_fourier)
    wfbd = np.zeros((NL, 256, 256), np.float32)
    for g in range(4):
        wfbd[:, g * 64:(g + 1) * 64, g * 64:(g + 1) * 64] = wf[:, g]
    shared["wfbd"] = wfbd
    for n, _, _ in CONST_SPECS:
        shared[n] = consts[n]
    cc = f(c_ctx)
    cdk_, cdv_, csk_, csv_, c_ = f(cache_diff_k), f(cache_diff_v), f(cache_swa_k), f(cache_swa_v), f(c)
    in_maps = []
    for i in range(8):
        m = dict(shared)
        m["xp"] = x_prompt[4 * i:4 * i + 4].reshape(T_P, D)
        m["xs"] = x_sample[i]
        m["cdk"] = cdk_[i].reshape(NL, NCTX, 512)
        m["cdv"] = cdv_[i].reshape(NL, NCTX, 512)
        m["csk"] = csk_[i].reshape(NL, NCTX, 128)
        m["csv"] = csv_[i].reshape(NL, NCTX, 128)
        m["cvec"] = np.ascontiguousarray(np.stack([fm(cc, 8), fm(c_[i], 8)], axis=-1))
        in_maps.append(m)
    return in_maps


def kernel(x_prompt, x_sample, cache_diff_k, cache_diff_v, cache_swa_k, cache_swa_v, c, c_ctx,
           w_ada, b_ada, norm_g, w_in, lam_q1, lam_k1, lam_q2, lam_k2, diff_norm_g, sink,
           w_fourier, w_out, final_g, _cfg=None, _ncores=8):
    cfg = _cfg or {}
    in_maps = make_in_maps(x_prompt, x_sample, cache_diff_k, cache_diff_v, cache_swa_k, cache_swa_v, c, c_ctx,
                           w_ada, b_ada, norm_g, w_in, lam_q1, lam_k1, lam_q2, lam_k2, diff_norm_g, sink,
                           w_fourier, w_out, final_g)[:_ncores]
    nc = build_nc(cfg)
    res = run_bass_kernel_spmd(nc, in_maps, core_ids=list(range(_ncores)))
    R = res.results
    f32 = np.float32
    y_prompt = np.concatenate([r["yp"].reshape(4, S_P, D) for r in R], axis=0).astype(f32)
    y_sample = np.stack([r["ys"] for r in R], axis=0).astype(f32)
    new_dk = np.concatenate([r["ndk"].reshape(4, NL, S_P, 4, 128) for r in R], axis=0).astype(f32)
    new_dv = np.concatenate([r["ndv"].reshape(4, NL, S_P, 4, 128) for r in R], axis=0).astype(f32)
    new_sk = np.concatenate([r["nsk"].reshape(4, NL, S_P, 2, 64) for r in R], axis=0).astype(f32)
    new_sv = np.concatenate([r["nsv"].reshape(4, NL, S_P, 2, 64) for r in R], axis=0).astype(f32)
    return (y_prompt, y_sample, new_dk, new_dv, new_sk, new_sv)
```

```python
import contextlib
import math

import ml_dtypes
import numpy as np

import concourse.bass as bass
import concourse.mybir as mybir
from concourse.bass_utils import run_bass_kernel_spmd

F32 = mybir.dt.float32
BF16 = mybir.dt.bfloat16
AF = mybir.ActivationFunctionType
ALU = mybir.AluOpType
AX = mybir.AxisListType

NL = 4
D = 1024
KD = 8
IN_COLS = 3328
AQ0, AK0, AV0, BQ0, BK0, BV0, UC0, Z0 = 0, 512, 1024, 1536, 1792, 1920, 2048, 2304
S_P, NSEQ_P, T_P = 256, 4, 1024
S_S, T_S, NCTX = 2048, 2048, 256
EPS = 1e-6
NEG = -30000.0
SCALE = 0.125


class Sem:
    def __init__(self, h, pool=None):
        self.h = h
        self.cnt = 0
        self.pool = pool


class Res:
    __slots__ = ("name", "w", "r", "ld", "st", "arena", "excl")

    def __init__(self, name, arena=False, excl=False):
        self.name = name
        self.arena = arena
        self.excl = excl
        self.w = None
        self.r = []
        self.ld = None
        self.st = None


class Op:
    __slots__ = ("eng", "fn", "kw", "deps", "is_dma", "signal", "cnt", "sem", "snap", "idx")

    def __init__(self, eng, fn, kw):
        self.eng = eng
        self.fn = fn
        self.kw = kw
        self.deps = []
        self.is_dma = False
        self.signal = False
        self.cnt = 0
        self.sem = None
        self.snap = None


class Prog:
    def __init__(self, nc, es):
        self.nc = nc
        self.ops = []
        self.engs = {"pe": nc.tensor, "act": nc.scalar, "dve": nc.vector, "pool": nc.gpsimd, "sp": nc.sync}
        self.esem = {e: Sem(es.enter_context(nc.semaphore("e_" + e))) for e in ("pe", "act", "dve", "pool")}
        self.free_sems = []
        self.free_sems.extend(Sem(es.enter_context(nc.semaphore(f"d{i}")), self.free_sems) for i in range(64))
        self.sw_sems = []
        self.sw_sems.extend(Sem(es.enter_context(nc.semaphore(f"w{i}")), self.sw_sems) for i in range(8))
        self.perm_sems = [Sem(es.enter_context(nc.semaphore(f"q{i}"))) for i in range(20)]
        self.st_sems = []
        self.marks = []

    def mark(self, label):
        self.marks.append((label, len(self.ops)))

    def op(self, eng, fn, kw, reads=(), writes=()):
        o = Op(eng, fn, kw)
        o.idx = len(self.ops)
        deps = []
        if any(R.excl for R in reads):
            writes = list(writes) + [R for R in reads if R.excl and R not in writes]
            reads = [R for R in reads if not R.excl]
        for R in reads:
            if R.w is not None:
                deps.append(R.w)
        for R in writes:
            if R.w is not None:
                deps.append(R.w)
            deps.extend(R.r)
        for e in deps:
            if e[0] == "c":
                if e[1].eng == eng and eng == "pe":
                    continue
                e[1].signal = True
            o.deps.append(e)
        ev = ("c", o)
        for R in writes:
            R.w = ev
            R.r = []
        for R in reads:
            if R.w is not ev:
                R.r = [e for e in R.r if not (e[0] == "c" and e[1].eng == eng)]
                R.r.append(ev)
        self.ops.append(o)
        return o

    def dma(self, q, out, in_, res, kind):
        engs = self.engs
        o = Op(q, engs[q].dma_start, dict(out=out, in_=in_))
        o.idx = len(self.ops)
        o.is_dma = True
        deps = []
        if res.w is not None:
            deps.append(res.w)
        if kind == "ld":
            deps.extend(res.r)
        for e in deps:
            if e[0] == "c":
                e[1].signal = True
            o.deps.append(e)
        if kind == "ld":
            if res.ld is None:
                res.ld = ((self.sw_sems if q == "pool" else self.free_sems) if res.arena else self.perm_sems).pop()
            sem = res.ld
        else:
            if res.st is None:
                res.st = (self.free_sems if res.arena else self.perm_sems).pop()
                self.st_sems.append(res.st)
            sem = res.st
        sem.cnt += 16
        ev = ("d", sem, sem.cnt)
        o.sem = sem
        if kind == "ld":
            res.w = ev
            res.r = []
        else:
            res.r.append(ev)
        self.ops.append(o)
        return o

    def retire(self, ress, new_ress):
        evs = []
        for R in ress:
            if R.w is not None:
                evs.append(R.w)
            evs.extend(R.r)
            for s in (R.ld, R.st):
                if s is not None:
                    s.pool.append(s)
            R.ld = R.st = None
        best = {}
        for e in evs:
            if e[0] == "c":
                key = ("c", e[1].eng)
                cur = best.get(key)
                if cur is None or cur[1].idx < e[1].idx:
                    best[key] = e
            else:
                key = ("d", id(e[1]))
                cur = best.get(key)
                if cur is None or cur[2] < e[2]:
                    best[key] = e
        evs = list(best.values())
        for R in new_ress:
            R.r = list(evs)

    def emit(self):
        seen = {e: {} for e in self.engs}
        cnt = {e: 0 for e in self.engs}
        nwait = 0
        self.inst_names = []
        for o in self.ops:
            eng = self.engs[o.eng]
            sn = seen[o.eng]
            need = {}
            for e in o.deps:
                if e[0] == "c":
                    s, v, snap = self.esem[e[1].eng], e[1].cnt, e[1].snap
                else:
                    s, v, snap = e[1], e[2], None
                cur = need.get(s)
                if cur is None or cur[0] < v:
                    need[s] = (v, snap)
            for s, (v, snap) in need.items():
                if sn.get(s, 0) < v:
                    eng.wait_ge(s.h, v)
                    nwait += 1
                    sn[s] = v
                if snap is not None:
                    for k2, v2 in snap.items():
                        if sn.get(k2, 0) < v2:
                            sn[k2] = v2
            inst = o.fn(**o.kw)
            self.inst_names.append((o.eng, getattr(getattr(inst, "ins", None), "name", None)))
            if o.is_dma:
                inst.then_inc(o.sem.h, 16)
            elif o.signal:
                cnt[o.eng] += 1
                o.cnt = cnt[o.eng]
                inst.then_inc(self.esem[o.eng].h, 1)
                o.snap = dict(sn)
                o.snap[self.esem[o.eng]] = o.cnt
        for s in self.st_sems:
            self.nc.sync.wait_ge(s.h, s.cnt)
        self.stats = (len(self.ops), nwait, dict(cnt))


class Buf:
    __slots__ = ("ap", "res")

    def __init__(self, ap, res):
        self.ap = ap
        self.res = res


def _bf(a):
    return np.ascontiguousarray(a.astype(ml_dtypes.bfloat16))


def make_consts():
    c = {}
    c["ident_bf"] = _bf(np.eye(128, dtype=np.float32))
    c["ident_f"] = np.eye(128, dtype=np.float32)
    c["ones_bf"] = _bf(np.ones((128, 128), np.float32))
    rm = np.zeros((128, 128), np.float32)
    for blk in range(2):
        o = blk * 64
        for i in range(16):
            rm[o + 16 + i, o + i] = -1.0
            rm[o + i, o + 16 + i] = 1.0
            rm[o + 48 + i, o + 32 + i] = -1.0
            rm[o + 32 + i, o + 48 + i] = 1.0
    c["rm"] = _bf(rm)
    t = np.arange(S_S)
    row = (t // 64).astype(np.float32)
    col = (t % 64).astype(np.float32)
    inv = (10000.0 ** (-np.arange(16, dtype=np.float32) / 16)).astype(np.float32)
    ar = row[:, None] * inv
    ac = col[:, None] * inv
    ang = np.concatenate([ar, ar, ac, ac], axis=-1)
    c["cosT"] = _bf(np.tile(np.cos(ang).T, (2, 1)))
    c["sinT"] = _bf(np.tile(np.sin(ang).T, (2, 1)))
    ka = np.arange(128)[:, None]
    qa = np.arange(128)[None, :]
    masks = np.zeros((128, 2, 128), np.float32)
    masks[:, 0, :] = np.where(ka <= qa, 1.0, 0.0)
    masks[:, 1, :] = np.where(qa <= ka, 1.0, 0.0)
    c["masks"] = _bf((1.0 - masks) * NEG)
    cc = np.arange(64)
    a = 2 * np.pi * np.outer(cc, cc) / 64.0
    ccbd = np.zeros((128, 2, 128), np.float32)
    for g in range(2):
        ccbd[g * 64:(g + 1) * 64, 0, g * 64:(g + 1) * 64] = np.cos(a) / 8.0
        ccbd[g * 64:(g + 1) * 64, 1, g * 64:(g + 1) * 64] = np.sin(a) / 8.0
    c["ccbd"] = ccbd
    def tables(S):
        s = np.arange(S, dtype=np.int64)
        m = np.outer(s, s) % S
        a = 2 * np.pi * m.astype(np.float64) / S
        return np.stack([np.cos(a), -np.sin(a)]) / math.sqrt(S)
    tp = tables(S_P)
    c["dftp"] = _bf(tp.reshape(2, 2, 128, 256).transpose(2, 0, 1, 3))
    ts = tables(S_S)
    c["dfts"] = _bf(ts.reshape(2, 16, 128, 8, 256).transpose(0, 3, 2, 1, 4))
    li = np.array([0.8 - 0.6 * math.exp(-0.3 * l) for l in range(NL)], np.float32)
    c["laminit"] = np.ascontiguousarray(np.broadcast_to(li[None, :], (128, NL))).astype(np.float32)
    c["omli"] = np.ascontiguousarray(1.0 - c["laminit"]).astype(np.float32)
    return c


CONST_SPECS = [
    ("ident_bf", [128, 128], BF16), ("ident_f", [128, 128], F32), ("ones_bf", [128, 128], BF16),
    ("rm", [128, 128], BF16), ("cosT", [128, 2048], BF16), ("sinT", [128, 2048], BF16),
    ("masks", [128, 2, 128], BF16), ("ccbd", [128, 2, 128], F32), ("dftp", [128, 2, 2, 256], BF16),
    ("laminit", [128, NL], F32), ("omli", [128, NL], F32),
]
PARAM_SPECS = [
    ("cvec", [128, 8, 2]), ("b_ada_fm", [128, NL, 24]), ("norm_g_fm", [128, NL, 8]), ("final_g_rep", [128, 1024]),
    ("lamv", [128, 4, NL, 64]), ("dgT", [128, NL]), ("sinkrep", [128, NL, 4]),
]


def build_nc(cfg):
    layers = cfg.get("layers", NL)
    do_p = cfg.get("prompt", True)
    do_s = cfg.get("sample", True)

    nc = bass.Bass("TRN2", target_bir_lowering=False)
    es = contextlib.ExitStack()

    def din(name, shape, dt=F32):
        return nc.dram_tensor(name, list(shape), dt, kind="ExternalInput").ap()

    def dout(name, shape):
        return nc.dram_tensor(name, list(shape), F32, kind="ExternalOutput").ap()

    xp = din("xp", [T_P, D])
    xs = din("xs", [T_S, D])
    cdk = din("cdk", [NL, NCTX, 512])
    cdv = din("cdv", [NL, NCTX, 512])
    csk = din("csk", [NL, NCTX, 128])
    csv = din("csv", [NL, NCTX, 128])
    w_in = din("w_in", [NL, D, IN_COLS])
    w_out = din("w_out", [NL, D, D])
    w_ada = din("w_ada", [NL, D, 3 * D])
    wfbd = din("wfbd", [NL, 256, 256])
    dfts = din("dfts", [2, 8, 128, 16, 256], BF16)
    cd = {n: din(n, s, dt) for n, s, dt in CONST_SPECS}
    pd = {n: din(n, s) for n, s in PARAM_SPECS}
    yp = dout("yp", [T_P, D])
    ys = dout("ys", [T_S, D])
    ndk = dout("ndk", [NSEQ_P, NL, S_P, 512])
    ndv = dout("ndv", [NSEQ_P, NL, S_P, 512])
    nsk = dout("nsk", [NSEQ_P, NL, S_P, 128])
    nsv = dout("nsv", [NSEQ_P, NL, S_P, 128])

    with es:
        P = Prog(nc, es)

        def sb(name, shape, dt):
            return es.enter_context(nc.sbuf_tensor(name, list(shape), dt))[:]

        xT = sb("xT", [128, 8, T_S], F32)
        hT = sb("hT", [128, 8, T_S], BF16)
        mixa = sb("mixa", [128, 4, T_S], BF16)
        mixbc = sb("mixbc", [128, 16, 512], BF16)
        wblk = [Buf(sb(f"wblk{i}", [128, 8, 512], BF16), Res(f"wblk{i}")) for i in range(2)]
        ct = {n: Buf(sb("c_" + n, s, dt), Res("c_" + n)) for n, s, dt in CONST_SPECS}
        ABm = Buf(sb("ABm", [128, 2, 2, 256], BF16), Res("ABm"))
        MOD = Buf(sb("MOD", [128, NL, 2, 3, 8], F32), Res("MOD"))
        LAM = Buf(sb("LAM", [128, NL], F32), Res("LAM"))
        DG = Buf(sb("DG", [128, NL], F32), Res("DG"))
        NLAM = Buf(sb("NLAM", [128, NL], F32), Res("NLAM"))
        sc_bf = Buf(sb("sc_bf", [128, 8, 2], BF16), Res("sc_bf"))
        modrow = Buf(sb("modrow", [2, 512], F32), Res("modrow"))
        modraw = Buf(sb("modraw", [128, 2, 24], F32), Res("modraw"))
        badat = Buf(sb("badat", [128, NL, 24], F32), Res("badat"))
        ngt = Buf(sb("ngt", [128, NL, 8], F32), Res("ngt"))
        ESK = Buf(sb("ESK", [128, NL, 4], F32), Res("ESK"))
        ARENA_N = 22784
        arena = sb("arena", [128, ARENA_N], BF16)
        psum = es.enter_context(nc.psum_tensor("ps", [128, 8, 512], F32))
        banks = [Buf(psum[:, b, :], Res(f"bank{b}", excl=True)) for b in range(8)]

        xT_res = [Res(f"xT{b}") for b in range(4)]
        hT_res = [Res(f"hT{b}") for b in range(4)]
        mix_res = [Res(f"mix{t}") for t in range(16)]
        mixa_res = [Res(f"mixa{b}") for b in range(4)]

        class Arena:
            def __init__(self):
                self.off = 0
                self.cur = []
                self.old = []

            def reset(self):
                self.old = self.old + self.cur
                self.cur = []
                self.off = 0

            def alloc(self, name, shape, dt):
                n = int(np.prod(shape[1:]))
                nel = n * (2 if dt == F32 else 1)
                pad = nel % 2
                assert self.off + nel + pad <= ARENA_N, (name, self.off, nel)
                ap = arena[:, self.off:self.off + nel]
                self.off += nel + pad
                if dt == F32:
                    ap = ap.bitcast(F32)
                if len(shape) == 3:
                    ap = ap.rearrange("p (a b) -> p a b", a=shape[1])
                elif len(shape) == 4:
                    ap = ap.rearrange("p (a b c) -> p a b c", a=shape[1], b=shape[2])
                r = Res(name, arena=True)
                self.cur.append(r)
                return Buf(ap, r)

            def commit(self):
                if self.cur:
                    P.retire(self.old, self.cur)
                    self.old = []

        ar = Arena()

        bank_rr = {"w": 0, "p": 0, "all": 0}
        bank_ids = {"w": [0, 1, 2, 3], "p": [7, 6, 0, 1, 2, 3], "all": [0, 1, 2, 3, 7, 6, 4, 5]}

        def nbank(pool):
            ids = bank_ids[pool]
            i = bank_rr[pool]
            bank_rr[pool] = i + 1
            return banks[ids[i % len(ids)]]

        wrr = [0]

        def load_w(src, ranges):
            slot = wblk[wrr[0] % 2]
            wrr[0] += 1
            sv = src.rearrange("(k p) c -> p k c", p=128)
            for (c0, n, d0) in ranges:
                P.dma("pool", slot.ap[:, :, d0:d0 + n], sv[:, :, c0:c0 + n], slot.res, "ld")
            return slot

        V = nc.vector
        A = nc.scalar
        PE = nc.tensor

        def mm(out, lhsT, rhs, start, stop, reads, writes, skip=False):
            kw = dict(out=out, lhsT=lhsT, rhs=rhs, start=start, stop=stop)
            if skip:
                kw["skip_group_check"] = True
            return P.op("pe", PE.matmul, kw, reads, writes)

        def tp(out, in_, identity, reads, writes):
            return P.op("pe", PE.transpose, dict(out=out, in_=in_, identity=identity), reads, writes)

        def dve_tt(out, in0, in1, op, reads, writes):
            return P.op("dve", V.tensor_tensor, dict(out=out, in0=in0, in1=in1, op=op), reads, writes)

        def dve_ts(out, in0, s1, s2, op0, op1, reads, writes):
            kw = dict(out=out, in0=in0, scalar1=s1, scalar2=s2, op0=op0)
            if op1 is not None:
                kw["op1"] = op1
            return P.op("dve", V.tensor_scalar, kw, reads, writes)

        def dve_stt(out, in0, scalar, in1, op0, op1, reads, writes):
            return P.op("dve", V.scalar_tensor_tensor, dict(out=out, in0=in0, scalar=scalar, in1=in1, op0=op0, op1=op1),
                        reads, writes)

        def dve_copy(out, in_, reads, writes):
            return P.op("dve", V.tensor_copy, dict(out=out, in_=in_), reads, writes)

        def dve_recip(out, in_, reads, writes):
            return P.op("dve", V.reciprocal, dict(out=out, in_=in_), reads, writes)

        def act(out, in_, func, reads, writes, scale=None, bias=None):
            kw = dict(out=out, in_=in_, func=func)
            if scale is not None:
                kw["scale"] = scale
            if bias is not None:
                kw["bias"] = bias
            return P.op("act", A.activation, kw, reads, writes)

        def act_copy(out, in_, reads, writes):
            return P.op("act", A.copy, dict(out=out, in_=in_), reads, writes)

        for n, _, _ in CONST_SPECS:
            P.dma("sp", ct[n].ap, cd[n], ct[n].res, "ld")
        ident_bf, ident_f, ones_bf, rm = ct["ident_bf"], ct["ident_f"], ct["ones_bf"], ct["rm"]

        P.dma("sp", badat.ap, pd["b_ada_fm"], badat.res, "ld")
        P.dma("sp", ngt.ap, pd["norm_g_fm"], ngt.res, "ld")
        ar.reset()
        pt = {n: ar.alloc("p_" + n, s, F32) for n, s in PARAM_SPECS if n in ("cvec", "lamv", "dgT", "sinkrep")}
        assert "final_g_rep" not in pt
        ltmp = ar.alloc("ltmp", [128, 2, NL, 64], F32)
        lsum = ar.alloc("lsum", [128, 2, NL], F32)
        sc = ar.alloc("sc", [128, 8, 2], F32)
        ar.commit()
        for n in pt:
            P.dma("sp", pt[n].ap, pd[n], pt[n].res, "ld")
        lamv = pt["lamv"]
        for i in range(2):
            dve_tt(ltmp.ap[:, i], lamv.ap[:, 2 * i], lamv.ap[:, 2 * i + 1], ALU.mult, [lamv.res], [ltmp.res])
        P.op("dve", V.reduce_sum, dict(out=lsum.ap, in_=ltmp.ap, axis=AX.X), [ltmp.res], [lsum.res])
        act(lsum.ap, lsum.ap, AF.Exp, [lsum.res], [lsum.res])
        dve_tt(LAM.ap, lsum.ap[:, 0, :], lsum.ap[:, 1, :], ALU.subtract, [lsum.res], [LAM.res])
        dve_tt(LAM.ap, LAM.ap, ct["laminit"].ap, ALU.add, [LAM.res, ct["laminit"].res], [LAM.res])
        dve_ts(NLAM.ap, LAM.ap, -1.0, None, ALU.mult, None, [LAM.res], [NLAM.res])
        dve_tt(DG.ap, pt["dgT"].ap, ct["omli"].ap, ALU.mult, [pt["dgT"].res, ct["omli"].res], [DG.res])
        act(ESK.ap, pt["sinkrep"].ap, AF.Exp, [pt["sinkrep"].res], [ESK.res])
        act(sc.ap, pt["cvec"].ap, AF.Silu, [pt["cvec"].res], [sc.res])
        dve_copy(sc_bf.ap, sc.ap, [sc.res], [sc_bf.res])

        mod_pending = [(l, blk) for l in range(layers) for blk in range(6)]

        def mod_step():
            if not mod_pending:
                return
            l, blk = mod_pending.pop(0)
            slot = load_w(w_ada[l], [(blk * 512, 512, 0)])
            bk = nbank("all")
            for k in range(KD):
                mm(bk.ap[0:2, :], sc_bf.ap[:, k, :], slot.ap[:, k, :], k == 0, k == KD - 1, [sc_bf.res, slot.res], [bk.res])
            dve_copy(modrow.ap, bk.ap[0:2, :], [bk.res], [modrow.res])
            bk2 = nbank("all")
            for jl in range(4):
                tp(bk2.ap[:, 2 * jl:2 * jl + 2], modrow.ap[0:2, jl * 128:(jl + 1) * 128], ident_f.ap[0:2, 0:2],
                   [modrow.res, ident_f.res], [bk2.res])
            dve_copy(modraw.ap[:, :, 4 * blk:4 * blk + 4], bk2.ap[:, 0:8].rearrange("p (j v) -> p v j", v=2),
                     [bk2.res], [modraw.res])
            if blk == 5:
                for v in range(2):
                    dve_tt(modraw.ap[:, v, :], modraw.ap[:, v, :], badat.ap[:, l, :], ALU.add,
                           [modraw.res, badat.res], [modraw.res])
                    dve_stt(MOD.ap[:, l, v, 0, :], modraw.ap[:, v, 8:16], 1.0, ngt.ap[:, l, :], ALU.add, ALU.mult,
                            [modraw.res, ngt.res], [MOD.res])
                    dve_copy(MOD.ap[:, l, v, 1, :], modraw.ap[:, v, 0:8], [modraw.res], [MOD.res])
                    dve_copy(MOD.ap[:, l, v, 2, :], modraw.ap[:, v, 16:24], [modraw.res], [MOD.res])


        def load_x(xd, T):
            ar.reset()
            xst = [ar.alloc(f"xst{i}", [128, 1024], F32) for i in range(2)]
            ar.commit()
            for tt in range(T // 128):
                st = xst[tt % 2]
                P.dma("sp", st.ap, xd[tt * 128:(tt + 1) * 128, :], st.res, "ld")
                for half in range(2):
                    bk = nbank("all")
                    for j in range(4):
                        k = half * 4 + j
                        tp(bk.ap[:, j * 128:(j + 1) * 128], st.ap[:, k * 128:(k + 1) * 128], ident_f.ap,
                           [st.res, ident_f.res], [bk.res])
                    dst = xT[:, half * 4:half * 4 + 4, tt * 128:(tt + 1) * 128]
                    src = bk.ap.rearrange("p (a b) -> p a b", a=4)
                    if half == 0:
                        dve_copy(dst, src, [bk.res], [xT_res[tt // 4]])
                    else:
                        act_copy(dst, src, [bk.res], [xT_res[tt // 4]])

        def norm_stage(T, g_ap, sh_ap, gres, final_out=None):
            ar.reset()
            nb_ = 2 if final_out is None else 1
            sqts = [ar.alloc(f"sqt{i}", [128, 8, 512], BF16) for i in range(nb_)]
            rstds = [ar.alloc(f"rstd{i}", [128, 512], F32) for i in range(nb_)]
            ntf = 4 if final_out is None else 2
            tmpf = [ar.alloc(f"tmpf{i}", [128, 512], F32) for i in range(ntf)]
            if final_out is not None:
                t2 = ar.alloc("t2", [128, 8, 512], F32)
                ost = [ar.alloc(f"ost{i}", [128, 1024], F32) for i in range(2)]
            ar.commit()
            nblk_ = T // 512
            bks = {}

            def stats(b):
                blk = slice(b * 512, (b + 1) * 512)
                sqt = sqts[b % nb_]
                act(sqt.ap, xT[:, :, blk], AF.Square, [xT_res[b]], [sqt.res])
                bk = nbank("all")
                for k in range(KD):
                    mm(bk.ap, ones_bf.ap, sqt.ap[:, k, :], k == 0, k == KD - 1, [ones_bf.res, sqt.res], [bk.res])
                bks[b] = bk

            stats(0)
            for b in range(nblk_):
                blk = slice(b * 512, (b + 1) * 512)
                rstd = rstds[b % nb_]
                if b + 1 < nblk_ and nb_ == 2:
                    stats(b + 1)
                bk = bks[b]
                act(rstd.ap, bk.ap, AF.Ln, [bk.res], [rstd.res], scale=1.0 / D, bias=EPS)
                act(rstd.ap, rstd.ap, AF.Exp, [rstd.res], [rstd.res], scale=-0.5)
                for k in range(KD):
                    tf = tmpf[k % ntf]
                    dve_tt(tf.ap, xT[:, k, blk], rstd.ap, ALU.mult, [xT_res[b], rstd.res], [tf.res])
                    if final_out is None and k % 2 == 1:
                        dve_ts(hT[:, k, blk], tf.ap, g_ap[:, k:k + 1], sh_ap[:, k:k + 1], ALU.mult, ALU.add,
                               [tf.res, gres], [hT_res[b]])
                    elif final_out is None:
                        act(hT[:, k, blk], tf.ap, AF.Identity, [tf.res, gres], [hT_res[b]],
                            scale=g_ap[:, k:k + 1], bias=sh_ap[:, k:k + 1])
                    else:
                        act(t2.ap[:, k, :], tf.ap, AF.Identity, [tf.res, gres], [t2.res], scale=g_ap[:, k:k + 1])
                if final_out is not None:
                    for tq in range(4):
                        tt = b * 4 + tq
                        os_ = ost[tt % 2]
                        for half in range(2):
                            bk2 = nbank("all")
                            for j in range(4):
                                k = half * 4 + j
                                tp(bk2.ap[:, j * 128:(j + 1) * 128], t2.ap[:, k, tq * 128:(tq + 1) * 128], ident_f.ap,
                                   [t2.res, ident_f.res], [bk2.res])
                            if half == 0:
                                dve_copy(os_.ap[:, 0:512], bk2.ap, [bk2.res], [os_.res])
                            else:
                                act_copy(os_.ap[:, 512:1024], bk2.ap, [bk2.res], [os_.res])
                        P.dma("sp", final_out[tt * 128:(tt + 1) * 128, :], os_.ap, os_.res, "st")
                if b + 1 < nblk_ and nb_ == 1:
                    stats(b + 1)

        def final_stage(T, y_out):
            ar.reset()
            gfull = ar.alloc("gfull", [128, 1024], F32)
            ost = [ar.alloc(f"ost{i}", [128, 1024], F32) for i in range(3)]
            junk = ar.alloc("junk", [128, 512], BF16)
            st4 = [ar.alloc(f"st4_{i}", [128, 4], F32) for i in range(3)]
            ar.commit()
            P.dma("sp", gfull.ap, pd["final_g_rep"], gfull.res, "ld")
            for tt in range(T // 128):
                os_, s4 = ost[tt % 3], st4[tt % 3]
                bkh = []
                for half in range(2):
                    bk = nbank("all")
                    for j in range(4):
                        k = half * 4 + j
                        tp(bk.ap[:, j * 128:(j + 1) * 128], xT[:, k, tt * 128:(tt + 1) * 128], ident_f.ap,
                           [xT_res[tt // 4], ident_f.res], [bk.res])
                    bkh.append(bk)
                    P.op("act", A.activation, dict(out=junk.ap, in_=bk.ap, func=AF.Square, accum_out=s4.ap[:, half:half + 1]),
                         [bk.res], [junk.res, s4.res])
                dve_tt(s4.ap[:, 2:3], s4.ap[:, 0:1], s4.ap[:, 1:2], ALU.add, [s4.res], [s4.res])
                act(s4.ap[:, 3:4], s4.ap[:, 2:3], AF.Ln, [s4.res], [s4.res], scale=1.0 / D, bias=EPS)
                act(s4.ap[:, 3:4], s4.ap[:, 3:4], AF.Exp, [s4.res], [s4.res], scale=-0.5)
                for half in range(2):
                    dve_stt(os_.ap[:, half * 512:(half + 1) * 512], bkh[half].ap, s4.ap[:, 3:4],
                            gfull.ap[:, half * 512:(half + 1) * 512], ALU.mult, ALU.mult,
                            [bkh[half].res, s4.res, gfull.res], [os_.res])
                P.dma("sp", y_out[tt * 128:(tt + 1) * 128, :], os_.ap, os_.res, "st")

        def proj_fm(slot, c0, b, n=512):
            bk = nbank("p")
            for k in range(KD):
                mm(bk.ap[:, 0:n], slot.ap[:, k, c0:c0 + 128], hT[:, k, b * 512:b * 512 + n], k == 0, k == KD - 1,
                   [slot.res, hT_res[b]], [bk.res])
            return bk

        def proj_tm(slot, c0, n, tt):
            bk = nbank("p")
            for k in range(KD):
                mm(bk.ap[:, 0:n], hT[:, k, tt * 128:(tt + 1) * 128], slot.ap[:, k, c0:c0 + n], k == 0, k == KD - 1,
                   [slot.res, hT_res[tt // 4]], [bk.res])
            return bk

        def proj_rope_pipelined(items, rp):
            bks = [None] * len(items)
            if items:
                bks[0] = proj_fm(items[0][0], items[0][1], items[0][2])
            for i, (slot_, c0, b, dst_ap, dst_res, tok0) in enumerate(items):
                if i + 1 < len(items):
                    bks[i + 1] = proj_fm(items[i + 1][0], items[i + 1][1], items[i + 1][2])
                rope_or_copy(bks[i], dst_ap, dst_res, tok0, 512, rp)

        def rope_or_copy(bk, dst_ap, dst_res, tok0, n, rp):
            if rp is None:
                act_copy(dst_ap, bk.ap[:, 0:n], [bk.res], [dst_res])
                return
            qpre, t1, t2, kk = rp
            qp = qpre[kk[0] % 2]
            kk[0] += 1
            act_copy(qp.ap[:, 0:n], bk.ap[:, 0:n], [bk.res], [qp.res])
            b2 = nbank("p")
            mm(b2.ap[:, 0:n], rm.ap, qp.ap[:, 0:n], True, True, [rm.res, qp.res], [b2.res])
            dve_tt(t1.ap[:, 0:n], b2.ap[:, 0:n], ct["sinT"].ap[:, tok0:tok0 + n], ALU.mult, [b2.res, ct["sinT"].res], [t1.res])
            dve_tt(t2.ap[:, 0:n], qp.ap[:, 0:n], ct["cosT"].ap[:, tok0:tok0 + n], ALU.mult, [qp.res, ct["cosT"].res], [t2.res])
            dve_tt(dst_ap, t1.ap[:, 0:n], t2.ap[:, 0:n], ALU.add, [t1.res, t2.res], [dst_res])

        STP = [(banks[0], banks[1]), (banks[2], banks[3])]
        STP3 = STP + [(banks[6], banks[7])]

        def run_attention(jobs, ET, nq):
            groups = []
            for ji, job in enumerate(jobs):
                job["started"] = set()
                job["accset"] = ji % 2
                gl = job["groups"]
                for gi, sts in enumerate(gl):
                    assert len(set(st[8] for st in sts)) == 1
                    groups.append((job, sts, gi == len(gl) - 1))
            n = len(groups)

            def acc_ap(job, slot):
                dv1 = job["dv1"]
                per = 512 // dv1
                bkb = banks[4 + job["accset"] + 2 * (slot // per)]
                c0 = (slot % per) * dv1
                return bkb, bkb.ap[:, c0:c0 + dv1]

            def QK(g):
                job, sts, last = groups[g]
                for si, (Qap, Kap, Kres, mask, Vap, Vres, outs, part, N) in enumerate(sts):
                    bk = STP3[g % 3][si]
                    mm(bk.ap[:, 0:N], Kap, Qap, True, True, [Kres, job["qres"]], [bk.res])
                    for pi, (i_, which) in enumerate(part):
                        mm(bk.ap[:, i_ * 128:(i_ + 1) * 128], ident_bf.ap, ct["masks"].ap[:, which, :], False,
                           pi == len(part) - 1, [ident_bf.res, ct["masks"].res], [bk.res], skip=True)

            def EXP(g):
                job, sts, last = groups[g]
                et = ET[g % 3]
                ns = len(sts)
                N = sts[0][8]
                b0 = (0, 2, 6)[g % 3]
                act(et.ap[:, 0:ns, 0:N], psum[:, b0:b0 + ns, 0:N], AF.Exp,
                    [STP3[g % 3][si].res for si in range(ns)], [et.res], scale=SCALE)

            def PV(g):
                job, sts, last = groups[g]
                et = ET[g % 3]
                for si, (Qap, Kap, Kres, mask, Vap, Vres, outs, part, N) in enumerate(sts):
                    for (slot, qt) in outs:
                        bkb, oap = acc_ap(job, slot)
                        first = id(bkb) not in job["started"]
                        job["started"].add(id(bkb))
                        mm(oap, et.ap[:, si, qt * 128:(qt + 1) * 128], Vap, first, last and si == len(sts) - 1,
                           [et.res, Vres], [bkb.res], skip=True)
                if last:
                    job["post"](job, lambda slot: acc_ap(job, slot))

            for g in range(min(2, n)):
                QK(g)
            for g in range(n):
                EXP(g)
                if g + 2 < n:
                    QK(g + 2)
                PV(g)

        def layer(l, T, S, nseq, sample, path, xout_k, xout_v, xout_sk, xout_sv):
            ntile = T // 128
            nblk = T // 512
            nst = S // 128
            nkt = nst + (2 if sample else 0)
            nq = min(512, S)
            nqt = nq // 128
            g_ap = MOD.ap[:, l, path, 0, :]
            sh_ap = MOD.ap[:, l, path, 1, :]
            gate_ap = MOD.ap[:, l, path, 2, :]

            P.mark(f"L{l}{'s' if sample else 'p'}:norm")
            norm_stage(T, g_ap, sh_ap, MOD.res)
            P.mark(f"L{l}{'s' if sample else 'p'}:fourier")

            if not sample:
                mod_step()
            ar.reset()
            u = [ar.alloc(f"u{t}", [128, 256], BF16) for t in range(ntile)]
            zsc = [ar.alloc(f"zsc{t}", [128, 256], BF16) for t in range(ntile)]
            YT = [ar.alloc(f"YT{i}", [128, 2, 2, 256], BF16) for i in range(2)]
            wfst = ar.alloc("wfst", [128, 2, 256], F32)
            if sample:
                ring = [ar.alloc(f"dft{i}", [128, 16, 256], BF16) for i in range(2)]
            ar.commit()
            P.dma("sp", wfst.ap, wfbd[l].rearrange("(j p) c -> p j c", p=128), wfst.res, "ld")
            for j in range(2):
                bk = nbank("all")
                for cs in range(2):
                    mm(bk.ap[:, cs * 256:(cs + 1) * 256], ct["ccbd"].ap[:, cs, :], wfst.ap[:, j, :], True, True,
                       [ct["ccbd"].res, wfst.res], [bk.res])
                act_copy(ABm.ap[:, j, :, :], bk.ap.rearrange("p (a b) -> p a b", a=2), [bk.res], [ABm.res])
            slot = load_w(w_in[l], [(UC0, 256, 0), (Z0 + 768, 256, 256)])
            for tt in range(ntile):
                bk = proj_tm(slot, 0, 512, tt)
                dve_copy(u[tt].ap, bk.ap[:, 0:256], [bk.res], [u[tt].res])
                act(zsc[tt].ap, bk.ap[:, 256:512], AF.Silu, [bk.res], [zsc[tt].res])
            rr = 0
            for seq in range(nseq):
                for sbk in range(S // 256):
                    yt = YT[(seq * (S // 256) + sbk) % 2]
                    for cs in range(2):
                        if sample:
                            tb = ring[rr % 2]
                            rr += 1
                            P.dma("sp", tb.ap, dfts[cs, sbk], tb.res, "ld")
                            tb_ap, tb_res = tb.ap, tb.res
                        else:
                            tb_ap, tb_res = ct["dftp"].ap[:, cs], ct["dftp"].res
                        bk = nbank("w")
                        for j in range(2):
                            for st in range(nst):
                                ut = u[seq * nst + st]
                                mm(bk.ap[:, j * 256:(j + 1) * 256], ut.ap[:, j * 128:(j + 1) * 128], tb_ap[:, st, :],
                                   st == 0, st == nst - 1, [ut.res, tb_res], [bk.res])
                        src = bk.ap.rearrange("p (a b) -> p a b", a=2)
                        if cs == 0:
                            act_copy(yt.ap[:, 0, :, :], src, [bk.res], [yt.res])
                        else:
                            dve_copy(yt.ap[:, 1, :, :], src, [bk.res], [yt.res])
                    for t2 in range(2):
                        tt = seq * nst + sbk * 2 + t2
                        bk = nbank("p")
                        i = 0
                        for cs in range(2):
                            for j in range(2):
                                mm(bk.ap[:, 0:256], yt.ap[:, cs, j, t2 * 128:(t2 + 1) * 128], ABm.ap[:, j, cs, :],
                                   i == 0, i == 3, [yt.res, ABm.res], [bk.res])
                                i += 1
                        dve_tt(mixbc[:, tt, 256:512], bk.ap[:, 0:256], zsc[tt].ap, ALU.mult, [bk.res, zsc[tt].res],
                               [mix_res[tt]])

            P.mark(f"L{l}{'s' if sample else 'p'}:swa")
            ar.reset()
            QTB = [ar.alloc(f"QTB{g}", [128, T], BF16) for g in range(2)]
            KTB = ar.alloc("KTB", [128, nseq * nkt * 128], BF16)
            V1B = ar.alloc("V1B", [128, nseq * nkt, 2, 65], BF16)
            zsb = [ar.alloc(f"zsb{t}", [128, 256], BF16) for t in range(ntile)]
            ET = [ar.alloc(f"ET{i}", [128, 2, 512], BF16) for i in range(3)]
            ptmp = ar.alloc("ptmp", [128, 8], F32)
            if sample:
                rp = ([ar.alloc(f"qpre{i}", [128, 512], BF16) for i in range(2)], ar.alloc("t1", [128, 512], F32),
                      ar.alloc("t2", [128, 512], F32), [0])
                cst = ar.alloc("cst", [128, 2, 128], F32)
            else:
                rp = None
                stg = [ar.alloc(f"stg{i}", [128, 256], F32) for i in range(3)]
            ar.commit()
            slot = load_w(w_in[l], [(BQ0, 64, 0), (BQ0 + 128, 64, 64), (BQ0 + 64, 64, 128), (BQ0 + 192, 64, 192),
                                    (BK0, 128, 256), (BV0, 128, 384)])
            slot2 = load_w(w_in[l], [(Z0 + 512, 256, 0)])
            P.op("dve", V.memset, dict(ap=V1B.ap[:, :, :, 64:65], constant=1.0), [], [V1B.res])
            items = []
            for b in range(nblk):
                for g in range(2):
                    items.append((slot, g * 128, b, QTB[g].ap[:, b * 512:(b + 1) * 512], QTB[g].res, b * 512))
                items.append((slot, 256, b, KTB.ap[:, b * 512:(b + 1) * 512], KTB.res, b * 512))
            proj_rope_pipelined(items, rp)
            for tt in range(ntile):
                bk = proj_tm(slot, 256, 256, tt)
                act_copy(V1B.ap[:, tt, :, 0:64], bk.ap[:, 128:256].rearrange("p (a b) -> p a b", a=2), [bk.res], [V1B.res])
                if not sample:
                    sg = stg[tt % 3]
                    seq, st = tt // nst, tt % nst
                    dve_copy(sg.ap, bk.ap[:, 0:256], [bk.res], [sg.res])
                    P.dma("sp", xout_sk[seq, l, st * 128:(st + 1) * 128, :], sg.ap[:, 0:128], sg.res, "st")
                    P.dma("sp", xout_sv[seq, l, st * 128:(st + 1) * 128, :], sg.ap[:, 128:256], sg.res, "st")
                bk = proj_tm(slot2, 0, 256, tt)
                act(zsb[tt].ap, bk.ap[:, 0:256], AF.Silu, [bk.res], [zsb[tt].res])
            if sample:
                P.dma("sp", cst.ap, csk[l].rearrange("(t p) c -> p t c", p=128), cst.res, "ld")
                bk = nbank("p")
                for t in range(2):
                    tp(bk.ap[:, t * 128:(t + 1) * 128], cst.ap[:, t, :], ident_f.ap, [cst.res, ident_f.res], [bk.res])
                act_copy(KTB.ap[:, S:S + 256], bk.ap[:, 0:256], [bk.res], [KTB.res])
                for t in range(2):
                    P.dma("pool", V1B.ap[:, nst + t, :, 0:64],
                          csv[l, t * 128:(t + 1) * 128, :].rearrange("p (a b) -> p a b", a=2), V1B.res, "ld")

            def make_post_b(hb, ptmp=ptmp, zsb=zsb):
                def post(job, acc):
                    nq_ = job["nqt"]
                    bkb, a0 = acc(0)
                    sums = bkb.ap[:, 0:nq_ * 65].rearrange("p (s c) -> p s c", c=65)[:, :, 64]
                    dve_ts(ptmp.ap[:, 0:nq_], sums, ESK.ap[:, l, hb:hb + 1], None, ALU.add, None,
                           [bkb.res, ESK.res], [ptmp.res])
                    dve_recip(ptmp.ap[:, 4:4 + nq_], ptmp.ap[:, 0:nq_], [ptmp.res], [ptmp.res])
                    for qt in range(nq_):
                        tt = job["tt0"] + qt
                        bkb, a_ = acc(qt)
                        dve_stt(mixbc[:, tt, hb * 64:(hb + 1) * 64], a_[:, 0:64], ptmp.ap[:, 4 + qt:5 + qt],
                                zsb[tt].ap[:, hb * 64:(hb + 1) * 64], ALU.mult, ALU.mult,
                                [bkb.res, ptmp.res, zsb[tt].res], [mix_res[tt]])
                return post

            jobs = []
            for seq in range(nseq):
                for hb in range(4):
                    g, kvh = hb % 2, hb // 2
                    rows = slice(kvh * 64, (kvh + 1) * 64)
                    for qb in range(S // nq):
                        q0 = seq * S + qb * nq

                        def kstep(j, lo, nt, masked, n0=0):
                            outs = [(lo + i, i) for i in range(nt)]
                            part = []
                            if masked:
                                part = [(i, 0 if n0 + lo + i == j - 1 else 1) for i in range(nt) if n0 + lo + i != j]
                            return (QTB[g].ap[rows, q0 + lo * 128:q0 + (lo + nt) * 128],
                                    KTB.ap[rows, j * 128:(j + 1) * 128], KTB.res, masked,
                                    V1B.ap[:, j, kvh, :], V1B.res, outs, part, nt * 128)

                        if sample:
                            n0 = qb * 4
                            gl = []
                            for ja, jb in ((n0 - 1, n0 + 4), (n0, n0 + 3), (n0 + 1, n0 + 2)):
                                grp = []
                                for j in (ja, jb):
                                    if 0 <= j < nst:
                                        lo = max(j - 1, n0) - n0
                                        hi = min(j + 1, n0 + 3) - n0
                                        grp.append(kstep(j, lo, hi - lo + 1, True, n0))
                                if grp:
                                    gl.append(grp)
                            gl.append([kstep(nst + t, 0, 4, False) for t in range(2)])
                        else:
                            gl = [[kstep(seq * nst + st, 0, nqt, False) for st in range(nst)]]
                        jobs.append(dict(qres=QTB[g].res, groups=gl, dv1=65, post=make_post_b(hb), nqt=nqt, tt0=q0 // 128))
            P.mark(f"L{l}{'s' if sample else 'p'}:swa_attn")
            run_attention(jobs, ET, nq)

            ar.reset()
            QT = ar.alloc("QT", [128, T], BF16)
            KT = ar.alloc("KT", [128, nseq * nkt * 128], BF16)
            VA = ar.alloc("VA", [128, nseq * nkt, 128], BF16)
            zsTs = [ar.alloc(f"zsT{i}", [128, T], BF16) for i in range(2)]
            ET = [ar.alloc(f"ET{i}", [128, 2, 512], BF16) for i in range(3)]

            r1 = ar.alloc("r1", [128, 512], F32)
            r2 = ar.alloc("r2", [128, 512], F32)
            la = ar.alloc("la", [128, 512], F32)
            lb = ar.alloc("lb", [128, 512], F32)
            sq = ar.alloc("sq", [128, 512], BF16)
            if sample:
                rp = ([ar.alloc(f"qpre{i}", [128, 512], BF16) for i in range(2)], ar.alloc("t1", [128, 512], F32),
                      ar.alloc("t2", [128, 512], F32), [0])
                cst = ar.alloc("cst", [128, 2, 128], F32)
            else:
                rp = None
                stg = [ar.alloc(f"stg{i}", [128, 256], F32) for i in range(3)]
            ar.commit()
            queue = []
            held = []
            for h in range(4):
                zsT = zsTs[h % 2]
                P.mark(f"L{l}{'s' if sample else 'p'}:diff{h}")
                if not sample:
                    mod_step()
                slot = load_w(w_in[l], [(AQ0 + h * 128, 128, 0), (AK0 + h * 128, 128, 128), (AV0 + h * 128, 128, 256),
                                        (Z0 + h * 128, 128, 384)])
                items = []
                for b in range(nblk):
                    blk = slice(b * 512, (b + 1) * 512)
                    items.append((slot, 0, b, QT.ap[:, blk], QT.res, b * 512))
                    items.append((slot, 128, b, KT.ap[:, blk], KT.res, b * 512))
                proj_rope_pipelined(items, rp)
                for tt in range(ntile):
                    if sample:
                        bk = proj_tm(slot, 256, 128, tt)
                        vo = 0
                    else:
                        bk = proj_tm(slot, 128, 256, tt)
                        vo = 128
                    act_copy(VA.ap[:, tt, :], bk.ap[:, vo:vo + 128], [bk.res], [VA.res])
                    if not sample:
                        sg = stg[tt % 3]
                        seq, st = tt // nst, tt % nst
                        dve_copy(sg.ap, bk.ap[:, 0:256], [bk.res], [sg.res])
                        P.dma("sp", xout_k[seq, l, st * 128:(st + 1) * 128, h * 128:(h + 1) * 128], sg.ap[:, 0:128],
                              sg.res, "st")
                        P.dma("sp", xout_v[seq, l, st * 128:(st + 1) * 128, h * 128:(h + 1) * 128], sg.ap[:, 128:256],
                              sg.res, "st")
                for b in range(nblk):
                    blk = slice(b * 512, (b + 1) * 512)
                    bk = proj_fm(slot, 384, b)
                    act(zsT.ap[:, blk], bk.ap, AF.Silu, [bk.res], [zsT.res])
                if sample:
                    P.dma("sp", cst.ap, cdk[l, :, h * 128:(h + 1) * 128].rearrange("(t p) c -> p t c", p=128), cst.res, "ld")
                    bk = nbank("p")
                    for t in range(2):
                        tp(bk.ap[:, t * 128:(t + 1) * 128], cst.ap[:, t, :], ident_f.ap, [cst.res, ident_f.res], [bk.res])
                    act_copy(KT.ap[:, S:S + 256], bk.ap[:, 0:256], [bk.res], [KT.res])
                    for t in range(2):
                        P.dma("pool", VA.ap[:, nst + t, :], cdv[l, t * 128:(t + 1) * 128, h * 128:(h + 1) * 128],
                              VA.res, "ld")

                P.mark(f"L{l}{'s' if sample else 'p'}:diff{h}attn")
                OT = [banks[4], banks[5]]
                SM = [banks[6], banks[7]]
                flat = []
                if sample:
                    for qb in range(S // nq):
                        for kt in range(nkt):
                            flat.append((qb * nq, 0, qb * nq, kt, kt, kt == 0, kt == nkt - 1))
                else:
                    for sp in range(nseq // 2):
                        for s2 in range(2):
                            seq = 2 * sp + s2
                            for kt in range(nkt):
                                flat.append((sp * 512, s2 * 256, seq * S, kt, seq * nkt + kt,
                                             s2 == 0 and kt == 0, s2 == 1 and kt == nkt - 1))
                n = len(flat)
                NQJ = 512

                def QK(g):
                    qj, co, q0, kt, j, fj, lj = flat[g]
                    for c in range(2):
                        rows = slice(c * 64, (c + 1) * 64)
                        bk = STP[g % 2][c]
                        mm(bk.ap[:, 0:nq], KT.ap[rows, j * 128:(j + 1) * 128], QT.ap[rows, q0:q0 + nq], True, True,
                           [KT.res, QT.res], [bk.res])

                def ep_free(q0):
                    dve_copy(r1.ap[:, 0:NQJ], OT[0].ap[:, 0:NQJ], [OT[0].res], [r1.res])
                    dve_copy(r2.ap[:, 0:NQJ], OT[1].ap[:, 0:NQJ], [OT[1].res], [r2.res])
                    dve_copy(la.ap[:, 0:NQJ], SM[0].ap[:, 0:NQJ], [SM[0].res], [la.res])
                    dve_copy(lb.ap[:, 0:NQJ], SM[1].ap[:, 0:NQJ], [SM[1].res], [lb.res])

                def ep_ln(buf):
                    return lambda q0, bk: act(buf.ap[:, 0:NQJ], buf.ap[:, 0:NQJ], AF.Ln, [buf.res], [buf.res])

                def ep_rcp(buf):
                    return lambda q0, bk: act(buf.ap[:, 0:NQJ], buf.ap[:, 0:NQJ], AF.Exp, [buf.res], [buf.res], scale=-1.0)

                def ep_comb(q0, bk):
                    dve_tt(r1.ap[:, 0:NQJ], r1.ap[:, 0:NQJ], la.ap[:, 0:NQJ], ALU.mult, [r1.res, la.res], [r1.res])
                    dve_tt(r2.ap[:, 0:NQJ], r2.ap[:, 0:NQJ], lb.ap[:, 0:NQJ], ALU.mult, [r2.res, lb.res], [r2.res])
                    dve_stt(r1.ap[:, 0:NQJ], r2.ap[:, 0:NQJ], NLAM.ap[:, l:l + 1], r1.ap[:, 0:NQJ], ALU.mult, ALU.add,
                            [r2.res, NLAM.res, r1.res], [r1.res])
                    dve_tt(sq.ap[:, 0:NQJ], r1.ap[:, 0:NQJ], r1.ap[:, 0:NQJ], ALU.mult, [r1.res], [sq.res])

                def ep_ssq(q0, bk):
                    mm(bk.ap[:, 0:NQJ], ones_bf.ap, sq.ap[:, 0:NQJ], True, True, [ones_bf.res, sq.res], [bk.res])
                    act(la.ap[:, 0:NQJ], bk.ap[:, 0:NQJ], AF.Ln, [bk.res], [la.res], scale=1.0 / 128, bias=EPS)

                def ep_fin(q0, bk, h=h, zsT=zsT):
                    act(la.ap[:, 0:NQJ], la.ap[:, 0:NQJ], AF.Exp, [la.res], [la.res], scale=-0.5)
                    dve_tt(r2.ap[:, 0:NQJ], r1.ap[:, 0:NQJ], la.ap[:, 0:NQJ], ALU.mult, [r1.res, la.res], [r2.res])
                    dve_stt(mixa[:, h, q0:q0 + NQJ], r2.ap[:, 0:NQJ], DG.ap[:, l:l + 1], zsT.ap[:, q0:q0 + NQJ],
                            ALU.mult, ALU.mult, [r2.res, DG.res, zsT.res], [mixa_res[q0 // 512]])

                stages = [ep_ln(la), ep_ln(lb), ep_rcp(la), ep_rcp(lb), ep_comb, ep_ssq, ep_fin]
                spread = nkt >= len(stages) + 2
                if n:
                    QK(0)
                for g in range(n):
                    qj, co, q0, kt, j, fj, lj = flat[g]
                    et = ET[g % 3]
                    b0_ = 2 * (g % 2)
                    act(et.ap[:, :, 0:nq], psum[:, b0_:b0_ + 2, 0:nq], AF.Exp, [STP[g % 2][0].res, STP[g % 2][1].res],
                        [et.res], scale=SCALE)
                    if queue:
                        fn, qq = queue.pop(0)
                        fn(qq, STP[g % 2][0] if fn is not ep_ssq else None) if fn is not ep_ssq else queue.insert(0, (fn, qq))
                    if g + 1 < n:
                        QK(g + 1)
                    for c in range(2):
                        mm(OT[c].ap[:, co:co + nq], VA.ap[:, j, :], et.ap[:, c, 0:nq], fj, lj,
                           [VA.res, et.res], [OT[c].res], skip=True)
                    for c in range(2):
                        mm(SM[c].ap[:, co:co + nq], ones_bf.ap, et.ap[:, c, 0:nq], fj, lj,
                           [ones_bf.res, et.res], [SM[c].res], skip=True)
                    if queue and queue[0][0] is ep_ssq:
                        fn, qq = queue.pop(0)
                        fn(qq, STP[g % 2][0])
                    if lj:
                        if not spread and held:
                            for fn in held:
                                fn(STP[g % 2][0])
                            held.clear()
                        ep_free(qj)
                        if spread:
                            queue.extend((fn, qj) for fn in stages)
                        else:
                            held.extend((lambda bk, fn=fn, qj=qj: fn(qj, bk)) for fn in stages)

            while queue:
                fn, qq = queue.pop(0)
                fn(qq, banks[0])
            for fn in held:
                fn(banks[0])
            held.clear()

            P.mark(f"L{l}{'s' if sample else 'p'}:outproj")
            if not sample:
                mod_step()
            for tt in range(ntile):
                bk = nbank("all")
                bkb = bk.ap.bitcast(BF16)
                for j in range(4):
                    tp(bkb[:, j * 128:(j + 1) * 128], mixbc[:, tt, j * 128:(j + 1) * 128], ident_bf.ap,
                       [mix_res[tt], ident_bf.res], [bk.res])
                dst = hT[:, 4:8, tt * 128:(tt + 1) * 128]
                srcv = bkb[:, 0:512].rearrange("p (a b) -> p a b", a=4)
                if tt % 2 == 0:
                    dve_copy(dst, srcv, [bk.res], [hT_res[tt // 4]])
                else:
                    act_copy(dst, srcv, [bk.res], [hT_res[tt // 4]])
            for cb in range(2):
                slot = load_w(w_out[l], [(cb * 512, 512, 0)])
                for dc in range(4):
                    d = cb * 4 + dc
                    for b in range(nblk):
                        blk = slice(b * 512, (b + 1) * 512)
                        bk = nbank("all")
                        for f in range(KD):
                            if f < 4:
                                rhs, rres = mixa[:, f, blk], mixa_res[b]
                            else:
                                rhs, rres = hT[:, f, blk], hT_res[b]
                            mm(bk.ap, slot.ap[:, f, dc * 128:(dc + 1) * 128], rhs, f == 0, f == KD - 1,
                               [slot.res, rres], [bk.res])
                        dve_stt(xT[:, d, blk], bk.ap, gate_ap[:, d:d + 1], xT[:, d, blk], ALU.mult, ALU.add,
                                [bk.res, MOD.res, xT_res[b]], [xT_res[b]])

        if do_p:
            load_x(xp, T_P)
        for _ in range(6):
            mod_step()
        if do_p:
            for l in range(layers):
                layer(l, T_P, S_P, NSEQ_P, False, 0, ndk, ndv, nsk, nsv)
        while mod_pending:
            mod_step()
        P.mark("final_p")
        if do_p:
            final_stage(T_P, yp)
        P.mark("sample_start")
        if do_s:
            load_x(xs, T_S)
            for l in range(layers):
                layer(l, T_S, S_S, 1, True, 1, None, None, None, None)
            final_stage(T_S, ys)

        P.emit()
        build_nc.stats = P.stats
        build_nc.marks = [(lab, idx, P.inst_names[idx:idx + 40]) for lab, idx in P.marks if idx < len(P.inst_names)]
    return nc


_CONSTS = None


def make_in_maps(x_prompt, x_sample, cache_diff_k, cache_diff_v, cache_swa_k, cache_swa_v, c, c_ctx,
                 w_ada, b_ada, norm_g, w_in, lam_q1, lam_k1, lam_q2, lam_k2, diff_norm_g, sink,
                 w_fourier, w_out, final_g):
    global _CONSTS
    f = lambda a: np.ascontiguousarray(np.asarray(a, dtype=np.float32))
    x_prompt, x_sample = f(x_prompt), f(x_sample)
    if _CONSTS is None:
        _CONSTS = make_consts()
    consts = _CONSTS

    def fm(v, n):
        return np.ascontiguousarray(f(v).reshape(n, 128).T)

    shared = {
        "w_in": f(w_in), "w_out": f(w_out), "w_ada": f(w_ada),
        "b_ada_fm": np.ascontiguousarray(f(b_ada).reshape(NL, 24, 128).transpose(2, 0, 1)),
        "norm_g_fm": np.ascontiguousarray(f(norm_g).reshape(NL, 8, 128).transpose(2, 0, 1)),
        "final_g_rep": np.ascontiguousarray(np.broadcast_to(f(final_g)[None, :], (128, D))),
        "lamv": np.ascontiguousarray(np.broadcast_to(
            np.stack([f(lam_q1), f(lam_k1), f(lam_q2), f(lam_k2)])[None], (128, 4, NL, 64))),
        "dgT": np.ascontiguousarray(f(diff_norm_g).T),
        "sinkrep": np.ascontiguousarray(np.broadcast_to(f(sink).reshape(NL, 4)[None], (128, NL, 4))),
        "dfts": consts["dfts"],
    }
    wf = f(w_fourier)
    wfbd = np.zeros((NL, 256, 256), np.float32)
    for g in range(4):
        wfbd[:, g * 64:(g + 1) * 64, g * 64:(g + 1) * 64] = wf[:, g]
    shared["wfbd"] = wfbd
    for n, _, _ in CONST_SPECS:
        shared[n] = consts[n]
    cc = f(c_ctx)
    cdk_, cdv_, csk_, csv_, c_ = f(cache_diff_k), f(cache_diff_v), f(cache_swa_k), f(cache_swa_v), f(c)
    in_maps = []
    for i in range(8):
        m = dict(shared)
        m["xp"] = x_prompt[4 * i:4 * i + 4].reshape(T_P, D)
        m["xs"] = x_sample[i]
        m["cdk"] = cdk_[i].reshape(NL, NCTX, 512)
        m["cdv"] = cdv_[i].reshape(NL, NCTX, 512)
        m["csk"] = csk_[i].reshape(NL, NCTX, 128)
        m["csv"] = csv_[i].reshape(NL, NCTX, 128)
        m["cvec"] = np.ascontiguousarray(np.stack([fm(cc, 8), fm(c_[i], 8)], axis=-1))
        in_maps.append(m)
    return in_maps


def kernel(x_prompt, x_sample, cache_diff_k, cache_diff_v, cache_swa_k, cache_swa_v, c, c_ctx,
           w_ada, b_ada, norm_g, w_in, lam_q1, lam_k1, lam_q2, lam_k2, diff_norm_g, sink,
           w_fourier, w_out, final_g, _cfg=None, _ncores=8):
    cfg = _cfg or {}
    in_maps = make_in_maps(x_prompt, x_sample, cache_diff_k, cache_diff_v, cache_swa_k, cache_swa_v, c, c_ctx,
                           w_ada, b_ada, norm_g, w_in, lam_q1, lam_k1, lam_q2, lam_k2, diff_norm_g, sink,
                           w_fourier, w_out, final_g)[:_ncores]
    nc = build_nc(cfg)
    res = run_bass_kernel_spmd(nc, in_maps, core_ids=list(range(_ncores)))
    R = res.results
    f32 = np.float32
    y_prompt = np.concatenate([r["yp"].reshape(4, S_P, D) for r in R], axis=0).astype(f32)
    y_sample = np.stack([r["ys"] for r in R], axis=0).astype(f32)
    new_dk = np.concatenate([r["ndk"].reshape(4, NL, S_P, 4, 128) for r in R], axis=0).astype(f32)
    new_dv = np.concatenate([r["ndv"].reshape(4, NL, S_P, 4, 128) for r in R], axis=0).astype(f32)
    new_sk = np.concatenate([r["nsk"].reshape(4, NL, S_P, 2, 64) for r in R], axis=0).astype(f32)
    new_sv = np.concatenate([r["nsv"].reshape(4, NL, S_P, 2, 64) for r in R], axis=0).astype(f32)
    return (y_prompt, y_sample, new_dk, new_dv, new_sk, new_sv)
```
